# Optimizing a Trainium2 kernel written in Bass

```python
import jax, jax.numpy as jnp
from jax import lax
import numpy as np

D_MODEL = 2048
BATCH = 4
SEQ = 4096
DEPTH = 1

CHUNK = 64
QBLOCK = 128
FOX_HEADS = 8
FOX_HEAD_DIM = D_MODEL // 16
FOX_WIDTH = FOX_HEADS * FOX_HEAD_DIM
MLSTM_HEADS = 4
MLSTM_V_DIM = D_MODEL // 8
MLSTM_QK_DIM = MLSTM_V_DIM // 2
MLSTM_WIDTH = MLSTM_HEADS * MLSTM_V_DIM
MLSTM_QK_WIDTH = MLSTM_HEADS * MLSTM_QK_DIM
CONV_WIDTH = 4
N_MEM = 256
XATTN_HEADS = 4
XATTN_HEAD_DIM = D_MODEL // XATTN_HEADS
D_FF = 4 * D_MODEL
EPS = 1e-6

IN_SPLITS = (FOX_WIDTH, FOX_WIDTH, FOX_WIDTH, FOX_HEADS,
             2 * MLSTM_QK_WIDTH, MLSTM_WIDTH, MLSTM_HEADS, MLSTM_HEADS,
             MLSTM_WIDTH)
IN_WIDTH = sum(IN_SPLITS)

kernel_name = "fox_mlstm_hybrid_streaming_block"


def _rms(x, g):
    x32 = x.astype(jnp.float32)
    y = x32 * lax.rsqrt(jnp.mean(x32 * x32, axis=-1, keepdims=True) + EPS)
    return (y * g.astype(jnp.float32)).astype(x.dtype)


def _to_heads(t, h):
    b, s, w = t.shape
    return t.reshape(b, s, h, w // h).transpose(0, 2, 1, 3)


def _from_heads(t):
    b, h, s, d = t.shape
    return t.transpose(0, 2, 1, 3)


def _causal_dwconv(u, w, b):
    c = u.shape[-1]
    y = lax.conv_general_dilated(u, w.astype(u.dtype)[:, None, :], window_strides=(1,),
                                 padding=[(CONV_WIDTH - 1, 0)],
                                 dimension_numbers=("NWC", "WIO", "NWC"),
                                 feature_group_count=c)
    return y + b.astype(u.dtype)


def _fox_attention(q, k, v, logf):
    s_len, d = q.shape[2], q.shape[3]
    scale = d ** -0.5
    c = lax.cumsum(logf, axis=2)
    outs = []
    for blk in range(s_len // QBLOCK):
        q0, q1 = blk * QBLOCK, (blk + 1) * QBLOCK
        qb, kb, vb = q[:, :, q0:q1], k[:, :, :q1], v[:, :, :q1]
        sc = jnp.einsum("bhqd,bhkd->bhqk", qb, kb).astype(jnp.float32) * scale
        sc = sc + c[:, :, q0:q1, None] - c[:, :, None, :q1]
        qpos = q0 + jnp.arange(QBLOCK)
        kpos = jnp.arange(q1)
        sc = jnp.where(qpos[:, None] >= kpos[None, :], sc, -jnp.inf)
        p = jax.nn.softmax(sc, axis=-1).astype(v.dtype)
        outs.append(jnp.einsum("bhqk,bhkd->bhqd", p, vb))
    return jnp.concatenate(outs, axis=2)


def _mlstm_chunkwise(q, k, v, ig, lf):
    bsz, nh, s_len, dk = q.shape
    dv = v.shape[-1]
    nc = s_len // CHUNK
    f32 = jnp.float32
    q = q.astype(f32)
    k = k.astype(f32) * (dk ** -0.5)
    v = v.astype(f32)

    def chunks(t):
        return jnp.moveaxis(t.reshape(bsz, nh, nc, CHUNK, *t.shape[3:]), 2, 0)

    tril = jnp.tril(jnp.ones((CHUNK, CHUNK), dtype=bool))

    def step(carry, inp):
        C, n, m = carry
        qc, kc, vc, ic, fc = inp
        b = jnp.cumsum(fc, axis=-1)
        dmat = b[..., :, None] - b[..., None, :] + ic[..., None, :]
        dmat = jnp.where(tril, dmat, -jnp.inf)
        inter = b + m[..., None]
        m_t = jnp.maximum(inter, jnp.max(dmat, axis=-1))
        w = jnp.exp(dmat - m_t[..., None])
        g = jnp.exp(inter - m_t)
        sqk = jnp.einsum("bhtd,bhsd->bhts", qc, kc) * w
        num = jnp.einsum("bhts,bhsv->bhtv", sqk, vc) + g[..., None] * jnp.einsum("bhvd,bhtd->bhtv", C, qc)
        den = jnp.sum(sqk, axis=-1) + g * jnp.einsum("bhd,bhtd->bht", n, qc)
        h = num / jnp.maximum(jnp.abs(den), jnp.exp(-m_t))[..., None]
        a = b[..., -1:] - b + ic
        m_new = jnp.maximum(b[..., -1] + m, jnp.max(a, axis=-1))
        wa = jnp.exp(a - m_new[..., None])
        decay = jnp.exp(b[..., -1] + m - m_new)
        C_new = decay[..., None, None] * C + jnp.einsum("bhs,bhsv,bhsd->bhvd", wa, vc, kc)
        n_new = decay[..., None] * n + jnp.einsum("bhs,bhsd->bhd", wa, kc)
        return (C_new, n_new, m_new), h

    init = (jnp.zeros((bsz, nh, dv, dk), f32), jnp.zeros((bsz, nh, dk), f32), jnp.zeros((bsz, nh), f32))
    _, hs = lax.scan(step, init, (chunks(q), chunks(k), chunks(v), chunks(ig), chunks(lf)))
    return jnp.moveaxis(hs, 0, 2).reshape(bsz, nh, s_len, dv)


def setup_inputs(seed: int = 0) -> dict:
    key = jax.random.key(seed)
    ks = jax.random.split(key, 24)
    nrm = jax.random.normal
    def gain(k, shape):
        return 1.0 + 0.02 * nrm(k, shape, jnp.float32)
    return {
        "x": nrm(ks[0], (BATCH, SEQ, D_MODEL), jnp.float32),
        "mem": nrm(ks[1], (BATCH, N_MEM, D_MODEL), jnp.float32),
        "mixer_norm": gain(ks[2], (DEPTH, D_MODEL)),
        "w_in": nrm(ks[3], (DEPTH, D_MODEL, IN_WIDTH), jnp.float32) * D_MODEL ** -0.5,
        "fox_f_bias": jax.random.uniform(ks[4], (DEPTH, FOX_HEADS), jnp.float32, 1.0, 6.0),
        "mlstm_i_bias": 0.1 * nrm(ks[5], (DEPTH, MLSTM_HEADS), jnp.float32),
        "mlstm_f_bias": jax.random.uniform(ks[6], (DEPTH, MLSTM_HEADS), jnp.float32, 3.0, 6.0),
        "conv_w": nrm(ks[7], (DEPTH, CONV_WIDTH, 2 * MLSTM_QK_WIDTH), jnp.float32) * CONV_WIDTH ** -0.5,
        "conv_b": 0.02 * nrm(ks[8], (DEPTH, 2 * MLSTM_QK_WIDTH), jnp.float32),
        "fox_q_norm": gain(ks[9], (DEPTH, FOX_HEAD_DIM)),
        "fox_k_norm": gain(ks[10], (DEPTH, FOX_HEAD_DIM)),
        "fox_out_norm": gain(ks[11], (DEPTH, FOX_WIDTH)),
        "mlstm_out_norm": gain(ks[12], (DEPTH, MLSTM_WIDTH)),
        "w_out": nrm(ks[13], (DEPTH, FOX_WIDTH + MLSTM_WIDTH, D_MODEL), jnp.float32) * (FOX_WIDTH + MLSTM_WIDTH) ** -0.5,
        "xattn_norm": gain(ks[14], (DEPTH, D_MODEL)),
        "mem_norm": gain(ks[15], (DEPTH, D_MODEL)),
        "w_xq": nrm(ks[16], (DEPTH, D_MODEL, D_MODEL), jnp.float32) * D_MODEL ** -0.5,
        "w_xkv": nrm(ks[17], (DEPTH, D_MODEL, 2 * D_MODEL), jnp.float32) * D_MODEL ** -0.5,
        "xq_norm": gain(ks[18], (DEPTH, XATTN_HEAD_DIM)),
        "xk_norm": gain(ks[19], (DEPTH, XATTN_HEAD_DIM)),
        "w_xo": nrm(ks[20], (DEPTH, D_MODEL, D_MODEL), jnp.float32) * D_MODEL ** -0.5,
        "mlp_norm": gain(ks[21], (DEPTH, D_MODEL)),
        "w_up": nrm(ks[22], (DEPTH, D_MODEL, D_FF), jnp.float32) * D_MODEL ** -0.5,
        "w_down": nrm(ks[23], (DEPTH, D_FF, D_MODEL), jnp.float32) * D_FF ** -0.5,
    }


def reference(x, mem, mixer_norm, w_in, fox_f_bias, mlstm_i_bias, mlstm_f_bias, conv_w, conv_b,
              fox_q_norm, fox_k_norm, fox_out_norm, mlstm_out_norm, w_out, xattn_norm, mem_norm,
              w_xq, w_xkv, xq_norm, xk_norm, w_xo, mlp_norm, w_up, w_down):
    bsz, s_len, _ = x.shape
    f32 = jnp.float32
    split_idx = list(np.cumsum(IN_SPLITS)[:-1])
    for l in range(DEPTH):
        xn = _rms(x, mixer_norm[l])
        proj = xn @ w_in[l]
        fq, fk, fv, ff, mqk, mv, mi, mf, mo = jnp.split(proj, split_idx, axis=-1)

        fq = _rms(_to_heads(fq, FOX_HEADS), fox_q_norm[l])
        fk = _rms(_to_heads(fk, FOX_HEADS), fox_k_norm[l])
        fv = _to_heads(fv, FOX_HEADS)
        logf = jax.nn.log_sigmoid(ff.astype(f32) + fox_f_bias[l].astype(f32)).transpose(0, 2, 1)
        fo = _from_heads(_fox_attention(fq, fk, fv, logf))
        fo = _rms(fo, fox_out_norm[l].reshape(FOX_HEADS, FOX_HEAD_DIM)).reshape(bsz, s_len, FOX_WIDTH)

        mqk = jax.nn.silu(_causal_dwconv(mqk, conv_w[l], conv_b[l]))
        mq, mk = jnp.split(mqk, 2, axis=-1)
        ig = (mi.astype(f32) + mlstm_i_bias[l].astype(f32)).transpose(0, 2, 1)
        lf = jax.nn.log_sigmoid(mf.astype(f32) + mlstm_f_bias[l].astype(f32)).transpose(0, 2, 1)
        mh = _mlstm_chunkwise(_to_heads(mq, MLSTM_HEADS), _to_heads(mk, MLSTM_HEADS),
                              _to_heads(mv, MLSTM_HEADS), ig, lf).astype(x.dtype)
        mh = _rms(_from_heads(mh), mlstm_out_norm[l].reshape(MLSTM_HEADS, MLSTM_V_DIM))
        mh = mh.reshape(bsz, s_len, MLSTM_WIDTH) * jax.nn.sigmoid(mo)

        x = x + jnp.concatenate([fo, mh], axis=-1) @ w_out[l]

        xn = _rms(x, xattn_norm[l])
        mn = _rms(mem, mem_norm[l])
        q = _rms(_to_heads(xn @ w_xq[l], XATTN_HEADS), xq_norm[l])
        mk_, mv_ = jnp.split(mn @ w_xkv[l], 2, axis=-1)
        k = _rms(_to_heads(mk_, XATTN_HEADS), xk_norm[l])
        v = _to_heads(mv_, XATTN_HEADS)
        sc = jnp.einsum("bhqd,bhkd->bhqk", q, k).astype(f32) * XATTN_HEAD_DIM ** -0.5
        p = jax.nn.softmax(sc, axis=-1).astype(v.dtype)
        co = _from_heads(jnp.einsum("bhqk,bhkd->bhqd", p, v)).reshape(bsz, s_len, D_MODEL)
        x = x + co @ w_xo[l]

        xn = _rms(x, mlp_norm[l])
        x = x + jnp.square(jax.nn.relu(xn @ w_up[l])) @ w_down[l]
    return x
```

```python
from contextlib import ExitStack
import numpy as np
import concourse.bass as bass
import concourse.mybir as mybir
from concourse.bass_utils import run_bass_kernel_spmd

F32 = mybir.dt.float32
BF16 = mybir.dt.bfloat16
AF = mybir.ActivationFunctionType
ALU = mybir.AluOpType

P = 128
D = 2048
KD = 16
TL = 4096
TO = 2048
NPV = 132
EPS = 1e-6
PMASK = -30000.0
LN_SC = -0.5 * float(np.log(128.0))

C_FQ, C_FK, C_FV, C_FF, C_MQK, C_MV, C_MI, C_MF, C_MO = 0, 1024, 2048, 3072, 3080, 4104, 5128, 5132, 5136
PV_MIX, PV_XAT, PV_MEM, PV_MLP, PV_FQ, PV_FK, PV_FO, PV_MO, PV_XQ, PV_XK, PV_CW, PV_CB, PV_FLAG, PV_PM = (
    0, 16, 32, 48, 64, 65, 66, 74, 82, 86, 90, 122, 130, 131)


class Buf:
    __slots__ = ("name", "w", "rs")

    def __init__(self, name=""):
        self.name = name
        self.w = None
        self.rs = []


class Ins:
    __slots__ = ("eng", "fn", "deps", "marked", "idx", "is_dma", "slot", "val", "n")

    def __init__(self, eng, fn, is_dma=False):
        self.eng = eng
        self.fn = fn
        self.deps = []
        self.marked = False
        self.idx = 0
        self.is_dma = is_dma
        self.slot = 0
        self.val = 0
        self.n = 0


class Sched:
    ENGS = ("pe", "act", "dve", "pool", "sp")
    NSLOT = 12

    def __init__(self):
        self.q = {e: [] for e in self.ENGS}
        self.ndma = {e: 0 for e in self.ENGS}
        self.all_dma = []
        self.pending = {e: [] for e in self.ENGS}
        self.last = {e: None for e in self.ENGS}
        self.lastdma = {}

    def _track(self, ins, reads, writes, nowaw=False):
        deps = ins.deps
        for b in reads:
            if b.w is not None:
                deps.append(b.w)
        for b in writes:
            if b.w is not None and not (nowaw and (not b.w.is_dma) and b.w.eng == ins.eng):
                deps.append(b.w)
            deps.extend(b.rs)
        for b in reads:
            rs = b.rs
            if not ins.is_dma:
                for i_ in range(len(rs)):
                    if (not rs[i_].is_dma) and rs[i_].eng == ins.eng:
                        rs[i_] = ins
                        break
                else:
                    rs.append(ins)
            else:
                rs.append(ins)
        for b in writes:
            b.w = ins
            b.rs = []

    def barrier(self):
        deps = [v for v in self.last.values() if v is not None] + list(self.lastdma.values())
        for e in self.ENGS:
            self.pending[e] = list(deps)

    def _pend(self, ins):
        if self.pending[ins.eng]:
            ins.deps.extend(self.pending[ins.eng])
            self.pending[ins.eng] = []
        if ins.is_dma:
            self.lastdma[(ins.eng, ins.slot)] = ins
        else:
            self.last[ins.eng] = ins

    def op(self, eng, fn, reads=(), writes=(), nowaw=False):
        ins = Ins(eng, fn)
        self._track(ins, reads, writes, nowaw)
        self._pend(ins)
        self.q[eng].append(ins)
        return ins

    def dma(self, eng, fn, reads=(), writes=()):
        ins = Ins(eng, fn, is_dma=True)
        self._track(ins, reads, writes)
        ins.n = self.ndma[eng]
        self.ndma[eng] += 1
        ins.slot = ins.n % self.NSLOT
        ins.val = 16 * (ins.n // self.NSLOT + 1)
        self._pend(ins)
        self.q[eng].append(ins)
        self.all_dma.append(ins)
        return ins

    def emit(self, nc, stack):
        for e in self.ENGS:
            for ins in self.q[e]:
                nd = []
                seen = set()
                for d in ins.deps:
                    if id(d) in seen:
                        continue
                    seen.add(id(d))
                    if not d.is_dma and d.eng == "pe" and ins.eng == "pe" and not ins.is_dma:
                        continue
                    nd.append(d)
                    if not d.is_dma:
                        d.marked = True
                ins.deps = nd
        for e in self.ENGS:
            c = 0
            for ins in self.q[e]:
                if not ins.is_dma and ins.marked:
                    c += 1
                    ins.idx = c
        esem = {e: stack.enter_context(nc.semaphore("s_" + e)) for e in self.ENGS}
        dsem = {}
        for e in self.ENGS:
            if self.ndma[e]:
                dsem[e] = [stack.enter_context(nc.semaphore("d_%s_%d" % (e, i))) for i in range(self.NSLOT)]
        block = stack.enter_context(nc.Block())
        sched = self

        def run(e, eng):
            waited = {}

            def wait(key, sem, val):
                if val <= 0:
                    return
                if waited.get(key, 0) >= val:
                    return
                waited[key] = val
                eng.wait_ge(sem, val)

            for ins in sched.q[e]:
                for d in ins.deps:
                    if d.is_dma:
                        wait((d.eng, d.slot), dsem[d.eng][d.slot], d.val)
                    else:
                        wait(d.eng, esem[d.eng], d.idx)
                if ins.is_dma:
                    wait((e, ins.slot), dsem[e][ins.slot], ins.val - 16)
                    ins.fn(eng).then_inc(dsem[e][ins.slot], 16)
                else:
                    r = ins.fn(eng)
                    if ins.marked:
                        r.then_inc(esem[e], 1)
            if e == "sp":
                last = {}
                for d in sched.all_dma:
                    last[(d.eng, d.slot)] = max(last.get((d.eng, d.slot), 0), d.val)
                for (de, sl), v in last.items():
                    wait((de, sl), dsem[de][sl], v)

        @block.tensor
        def _(eng):
            run("pe", eng)

        @block.scalar
        def _(eng):
            run("act", eng)

        @block.vector
        def _(eng):
            run("dve", eng)

        @block.gpsimd
        def _(eng):
            run("pool", eng)

        @block.sync
        def _(eng):
            run("sp", eng)


class DeferS:
    def __init__(self, S):
        self.S = S
        self.q = []

    def op(self, *a, **k):
        self.q.append(("op", a, k))

    def dma(self, *a, **k):
        self.q.append(("dma", a, k))

    def drain(self, n=None):
        n = len(self.q) if n is None else min(n, len(self.q))
        for _ in range(n):
            kind, a, k = self.q.pop(0)
            getattr(self.S, kind)(*a, **k)


class Builder:
    def __init__(self, stop_after=None, dbg=()):
        self.stop_after = stop_after
        self.dbg = set(dbg)
        self.nc = bass.Bass("TRN2", target_bir_lowering=False)
        self.S = Sched()
        self.wn = 0

    def dram(self, name, shape, dt, kind=None):
        if kind is None:
            kind = "ExternalOutput" if name in self.dbg else "Internal"
        return self.nc.dram_tensor(name, list(shape), dt, kind=kind).ap()

    def view(self, off, shape, dt):
        esz = 2 if dt == BF16 else 4
        n = 1
        for s in shape[1:]:
            n *= s
        nbytes = n * esz
        assert off % 4 == 0 and nbytes % 4 == 0
        assert off + nbytes <= self.arena_bytes, (off, nbytes)
        v = self.arena[:, off // 4:(off + nbytes) // 4]
        if dt == BF16:
            v = v.bitcast(BF16)
        if len(shape) == 3:
            v = v.rearrange("p (a b) -> p a b", a=shape[1])
        elif len(shape) == 4:
            v = v.rearrange("p (a b c) -> p a b c", a=shape[1], b=shape[2])
        if shape[0] < P:
            v = v[0:shape[0]]
        return v

    def load_w(self, src, r0, nkc, c0, ncols):
        S = self.S
        sl = self.wn % self.NW
        self.wn += 1
        wt = self.wring[sl]
        wb = self.wB[sl]
        step = 4 if ncols >= 256 else nkc
        for q in range(0, nkc, step):
            n = min(step, nkc - q)
            S.dma("pool", lambda e, q=q, n=n: e.dma_start(
                out=wt[:, q:q + n, 0:ncols],
                in_=src[r0 + q * P:r0 + (q + n) * P, c0:c0 + ncols].rearrange("(kc p) c -> p kc c", p=P)),
                writes=[wb])
        return wt, wb

    def rms_T(self, src3, srcB_fn, n, gcol, dst_fn, dstB_fn, scale):
        S = self.S
        KC = src3.shape[1]
        pbn, pbnB = self.next_nbank()
        lnb, lnbB = self.tmpf[0], self.tmpfB[0]
        rs, rsB = self.tmpf[1], self.tmpfB[1]
        for kc in range(KC):
            sqh, sqhB = self.sqh[self.sqn % 2], self.sqhB[self.sqn % 2]
            self.sqn += 1
            S.op("act", lambda e, kc=kc, sqh=sqh: e.activation(out=sqh[:, 0:n], in_=src3[:, kc, :], func=AF.Square), reads=srcB_fn(kc), writes=[sqhB])
            S.op("pe", lambda e, kc=kc, sqh=sqh: e.matmul(pbn[:, 0:n], lhsT=self.ones[:], rhs=sqh[:, 0:n], start=(kc == 0), stop=(kc == KC - 1)),
                 reads=[sqhB], writes=[pbnB])
        S.op("act", lambda e: e.activation(out=lnb[:, 0:n], in_=pbn[:, 0:n], func=AF.Ln, bias=self.epsc[:, 0:1], scale=scale), reads=[pbnB], writes=[lnbB])
        S.op("act", lambda e: e.activation(out=rs[:, 0:n], in_=lnb[:, 0:n], func=AF.Exp, scale=-0.5), reads=[lnbB], writes=[rsB])
        for kc in range(KC):
            S.op("dve", lambda e, kc=kc: e.scalar_tensor_tensor(out=dst_fn(kc), in0=src3[:, kc, :], scalar=self.pv[:, gcol + kc:gcol + kc + 1],
                                                                 in1=rs[:, 0:n], op0=ALU.mult, op1=ALU.mult),
                 reads=list(srcB_fn(kc)) + [rsB], writes=dstB_fn(kc))

    def next_nbank(self):
        i = self.nbase + (self.nb % self.nmod)
        self.nb += 1
        return self.pb[i], self.pbB[i]

    def next_pbank(self):
        i = self.pbn % self.pring
        self.pbn += 1
        return self.pb[i], self.pbB[i]

    def next_stage(self):
        i = self.stn % len(self.stage)
        self.stn += 1
        return self.stage[i], self.stageB[i]

    def headnorm(self, ps, psB, n, gcol, dst, dstB, scale=1.0 / 128, sq=None, sqB=None, S=None):
        S = self.S if S is None else S
        if sq is None:
            sqh, sqhB = self.sqh[self.sqn % 2], self.sqhB[self.sqn % 2]
            self.sqn += 1
        else:
            sqh, sqhB = sq, sqB
        pbn, pbnB = self.next_nbank()
        lnb, lnbB = self.tmpf[2], self.tmpfB[2]
        rs, rsB = self.tmpf[3], self.tmpfB[3]
        S.op("act", lambda e: e.activation(out=sqh[:, 0:n], in_=ps, func=AF.Square), reads=[psB], writes=[sqhB])
        S.op("pe", lambda e: e.matmul(pbn[:, 0:n], lhsT=self.ones[:], rhs=sqh[:, 0:n], start=True, stop=True), reads=[sqhB], writes=[pbnB])
        S.op("act", lambda e: e.activation(out=lnb[:, 0:n], in_=pbn[:, 0:n], func=AF.Ln, bias=self.epsc[:, 0:1], scale=scale), reads=[pbnB], writes=[lnbB])
        S.op("act", lambda e: e.activation(out=rs[:, 0:n], in_=lnb[:, 0:n], func=AF.Exp, scale=-0.5), reads=[lnbB], writes=[rsB])
        S.op("dve", lambda e: e.scalar_tensor_tensor(out=dst, in0=ps, scalar=self.pv[:, gcol:gcol + 1], in1=rs[:, 0:n], op0=ALU.mult, op1=ALU.mult),
             reads=[psB, rsB], writes=[dstB])

    def build(self):
        nc, S = self.nc, self.S
        inp = {}

        def din(name, shape):
            inp[name] = nc.dram_tensor(name, list(shape), F32, kind="ExternalInput").ap()
            return inp[name]

        xT = din("xT", [D, TL])
        memT = din("memT", [D, 256])
        w_in = din("w_in", [D, 6160])
        w_out = din("w_out", [D, D])
        w_xq = din("w_xq", [D, D])
        w_xkv = din("w_xkv", [D, 2 * D])
        w_xo = din("w_xo", [D, D])
        w_up = din("w_up", [D, 4 * D])
        w_down = din("w_down", [4 * D, D])
        pvd = din("pv", [P, NPV])
        gbd = din("gb", [8, 4])
        outT = nc.dram_tensor("outT", [D, TO], F32, kind="ExternalOutput").ap()

        qfT = self.dram("qfT", [8, P, TO], BF16)
        kfT = self.dram("kfT", [8, P, TL], BF16)
        vf = self.dram("vf", [8, P, 32, 128], BF16)
        mqT = self.dram("mqT", [4, P, TO], BF16)
        mkT = self.dram("mkT", [4, P, TL], BF16)
        mvv = self.dram("mvv", [4, P, 32, 256], BF16)
        mosT = self.dram("mosT", [8, P, TO], BF16)
        gT = self.dram("gT", [3, 8, TL], F32)
        Etab = self.dram("Etab", [8, P, 512], F32)
        Wtab = self.dram("Wtab", [4, P, 512], F32)
        lamem = self.dram("lamem", [4, 2, TO], F32)
        catT = self.dram("catT", [D, TO], BF16) if "catT" in self.dbg else None
        x1T = self.dram("x1T", [D, TO], F32) if "x1T" in self.dbg else None
        x2T = self.dram("x2T", [D, TO], F32) if "x2T" in self.dbg else None
        qfB, kfB, vfB, mqB, mkB, mvB, mosB, gTB, EtB, WtB, lmB = [Buf() for _ in range(11)]

        st = ExitStack()
        self.st = st
        self.arena_bytes = 206 * 1024
        self.arena = st.enter_context(nc.sbuf_tensor("arena", [P, self.arena_bytes // 4], F32))
        self.pb = [st.enter_context(nc.psum_tensor("pb%d" % i, [P, 512], F32)) for i in range(8)]
        self.pbB = [Buf("pb%d" % i) for i in range(8)]
        self.nb = 0
        self.nbase = 4
        self.nmod = 2
        self.srecent = []
        self.pring = 4
        self.pbn = 0
        self.stn = 0
        self.sqn = 0
        pb, pbB = self.pb, self.pbB

        off = 0

        def take(nbytes):
            nonlocal off
            o = off
            off += (nbytes + 31) // 32 * 32
            return o

        self.ones = self.view(take(256), [P, 128], BF16)
        ident = self.view(take(512), [P, 128], F32)
        tri = self.view(take(256), [P, 128], BF16)
        self.pv = self.view(take(NPV * 4), [P, NPV], F32)
        pv = self.pv
        self.epsc = self.view(take(32), [P, 8], F32)
        gb = self.view(take(16), [8, 4], F32)
        sel = self.view(take(8 * 128 * 4), [8, 8, 128], F32)
        halo = self.view(take(8 * 4 * 4), [P, 8, 4], F32)
        self.NW = 2
        self.wring = [self.view(take(16 * 512 * 2), [P, 16, 512], BF16) for _ in range(self.NW)]
        self.wB = [Buf("w%d" % i) for i in range(self.NW)]
        kmem = self.view(take(16 * 256 * 2), [P, 16, 256], BF16)
        vmem = self.view(take(2 * 2048 * 2), [P, 2, 2048], BF16)
        kmemB = [Buf() for _ in range(4)]
        vmemB = Buf()
        tmpf_off = [take(2048) for _ in range(9)]
        self.tmpf = [self.view(o_, [P, 512], F32) for o_ in tmpf_off]
        xb16 = [self.view(tmpf_off[i_], [P, 512], BF16) for i_ in range(2)]
        self.tmpfB = [Buf("tmpf%d" % i) for i in range(9)]
        self.sqh = [self.view(take(1024), [P, 512], BF16) for _ in range(2)]
        self.sqhB = [Buf() for _ in range(2)]
        self.stage = [self.view(take(1024), [P, 512], BF16) for _ in range(4)]
        self.stageB = [Buf() for _ in range(4)]
        PERS = off
        tmpf, tmpfB = self.tmpf, self.tmpfB

        cI, cT = Buf("ident"), Buf("tri")
        S.op("pool", lambda e: e.memset(self.ones[:], 1.0))
        S.op("pool", lambda e: e.memset(ident[:], 0.0), writes=[cI])
        S.op("pool", lambda e: e.affine_select(out=ident[:], in_=ident[:], pattern=[[-1, 128]], compare_op=ALU.not_equal, fill=1.0,
                                               base=0, channel_multiplier=1), reads=[cI], writes=[cI])
        S.op("pool", lambda e: e.memset(tri[:], 1.0), writes=[cT])
        S.op("pool", lambda e: e.affine_select(out=tri[:], in_=tri[:], pattern=[[1, 128]], compare_op=ALU.is_ge, fill=0.0,
                                               base=0, channel_multiplier=-1), reads=[cT], writes=[cT])
        S.op("pool", lambda e: e.memset(self.epsc[:], EPS))
        S.op("pool", lambda e: e.memset(halo[:], 0.0))
        S.dma("sp", lambda e: e.dma_start(out=pv[:], in_=pvd[:, :]))
        S.dma("sp", lambda e: e.dma_start(out=gb[:], in_=gbd[:, :]))
        S.barrier()
        S.op("dve", lambda e: e.tensor_copy(out=sel[:], in_=ident[0:8, 0:8].unsqueeze(2).to_broadcast([8, 8, 128])))
        S.barrier()

        offA = PERS
        XN = self.view(offA, [P, 16, 2048], BF16)
        offA += 16 * 2048 * 2
        XNB = [[Buf() for _ in range(16)] for _ in range(8)]
        xs = [self.view(offA + i * 16384, [P, 16, 256], F32) for i in range(2)]
        xsB = [Buf() for _ in range(2)]
        offA += 2 * 16384
        ub = [self.view(offA + i * 2064, [P, 516], F32) for i in range(2)]
        ubB = [Buf() for _ in range(2)]
        offA += 2 * 2064
        accb = [self.view(offA + i * 2048, [P, 512], F32) for i in range(2)]
        accB = [Buf() for _ in range(2)]
        offA += 2 * 2048
        gst = [self.view(offA + i * 2048, [8, 512], F32) for i in range(2)]
        gstB = [Buf() for _ in range(2)]
        offA += 2 * 2048
        wg = self.view(offA, [P, 16, 16], BF16)
        wgB = Buf("wg")
        offA += 512
        haloB = [Buf() for _ in range(8)]
        assert offA <= self.arena_bytes, offA

        def xn_tile_reads(t):
            return [XNB[2 * t + s][kc] for s in range(2) for kc in range(16)]

        def inproj_pass(tok0, own):
            for sub in range(8):
                xsl, xslB = xs[sub % 2], xsB[sub % 2]
                t0 = tok0 + sub * 256
                for q in range(4):
                    S.dma("sp", lambda e, q=q, xsl=xsl, t0=t0: e.dma_start(
                        out=xsl[:, q * 4:(q + 1) * 4, :],
                        in_=xT[q * 512:(q + 1) * 512, t0:t0 + 256].rearrange("(kc p) t -> p kc t", p=P)), writes=[xslB])
                self.rms_T(xsl[:, :, :], lambda kc, xslB=xslB: [xslB], 256, PV_MIX,
                           lambda kc, sub=sub: XN[:, kc, sub * 256:(sub + 1) * 256],
                           lambda kc, sub=sub: [XNB[sub][kc]], 1.0 / D)

            groups = []
            if own:
                groups += [("fq", 0), ("fq", 1)]
            groups += [("fk", 0), ("fk", 1), ("fv", 0), ("fv", 1), ("mqk", 0), ("mqk", 1), ("mv", 0), ("mv", 1)]
            if own:
                groups += [("mo", 0), ("mo", 1)]
            cbase = {"fq": C_FQ, "fk": C_FK, "fv": C_FV, "mqk": C_MQK, "mv": C_MV, "mo": C_MO}
            for kind, gi in groups:
                wt, wb = self.load_w(w_in, 0, 16, cbase[kind] + gi * 512, 512)
                tiles = range(4)
                if kind == "mqk" and gi == 0 and not own:
                    tiles = [3]
                for t in tiles:
                    lt0 = tok0 + t * 512
                    ot0 = t * 512
                    if kind in ("fv", "mv"):
                        for bl in range(4):
                            ps, psB = self.next_pbank()
                            c_lo = t * 512 + bl * 128
                            for kc in range(16):
                                S.op("pe", lambda e, kc=kc, ps=ps, wt=wt, c_lo=c_lo: e.matmul(
                                    ps[:, :], lhsT=XN[:, kc, c_lo:c_lo + 128], rhs=wt[:, kc, :], start=(kc == 0), stop=(kc == 15)),
                                    reads=[wb, XNB[c_lo // 256][kc]], writes=[psB])
                            sg, sgB = self.next_stage()
                            S.op("act", lambda e, ps=ps, sg=sg: e.activation(out=sg[:], in_=ps[:], func=AF.Copy), reads=[psB], writes=[sgB])
                            kb = (lt0 + bl * 128) // 128
                            if kind == "fv":
                                S.dma("sp", lambda e, sg=sg, kb=kb, gi=gi: e.dma_start(
                                    out=vf[4 * gi:4 * gi + 4, :, kb, :].rearrange("h p d -> p h d"),
                                    in_=sg[:].rearrange("p (h d) -> p h d", h=4)), reads=[sgB], writes=[vfB])
                            else:
                                S.dma("sp", lambda e, sg=sg, kb=kb, gi=gi: e.dma_start(
                                    out=mvv[2 * gi:2 * gi + 2, :, kb, :].rearrange("h p d -> p h d"),
                                    in_=sg[:].rearrange("p (h d) -> p h d", h=2)), reads=[sgB], writes=[mvB])
                        continue
                    for c in range(4):
                        ps, psB = self.next_pbank()
                        for kc in range(16):
                            S.op("pe", lambda e, kc=kc, ps=ps, wt=wt, c=c, t=t: e.matmul(
                                ps[:, :], lhsT=wt[:, kc, c * 128:(c + 1) * 128], rhs=XN[:, kc, t * 512:(t + 1) * 512], start=(kc == 0), stop=(kc == 15)),
                                reads=[wb, XNB[2 * t][kc], XNB[2 * t + 1][kc]], writes=[psB])
                        sg, sgB = self.next_stage()
                        hh = gi * 4 + c
                        if kind == "fq":
                            self.headnorm(ps[:, :], psB, 512, PV_FQ, sg[:], sgB)
                            S.dma("sp", lambda e, sg=sg, hh=hh, ot0=ot0: e.dma_start(out=qfT[hh, :, ot0:ot0 + 512], in_=sg[:]), reads=[sgB], writes=[qfB])
                        elif kind == "fk":
                            self.headnorm(ps[:, :], psB, 512, PV_FK, sg[:], sgB)
                            S.dma("sp", lambda e, sg=sg, hh=hh, lt0=lt0: e.dma_start(out=kfT[hh, :, lt0:lt0 + 512], in_=sg[:]), reads=[sgB], writes=[kfB])
                        elif kind == "mo":
                            S.op("act", lambda e, ps=ps, sg=sg: e.activation(out=sg[:], in_=ps[:], func=AF.Sigmoid), reads=[psB], writes=[sgB])
                            S.dma("sp", lambda e, sg=sg, hh=hh, ot0=ot0: e.dma_start(out=mosT[hh, :, ot0:ot0 + 512], in_=sg[:]), reads=[sgB], writes=[mosB])
                        else:
                            ch = gi * 4 + c
                            u, uB = ub[ch % 2], ubB[ch % 2]
                            ac, acB = accb[ch % 2], accB[ch % 2]
                            S.op("dve", lambda e, u=u, ch=ch: e.tensor_copy(out=u[:, 0:3], in_=halo[:, ch, 0:3]), reads=[haloB[ch]], writes=[uB])
                            S.op("act", lambda e, u=u, ps=ps: e.activation(out=u[:, 3:515], in_=ps[:], func=AF.Copy), reads=[psB], writes=[uB])
                            S.op("dve", lambda e, u=u, ch=ch: e.tensor_copy(out=halo[:, ch, 0:3], in_=u[:, 512:515]), reads=[uB], writes=[haloB[ch]])
                            S.op("dve", lambda e, u=u, ac=ac, ch=ch: e.tensor_scalar(
                                out=ac[:], in0=u[:, 0:512], scalar1=pv[:, PV_CW + ch:PV_CW + ch + 1], scalar2=pv[:, PV_CB + ch:PV_CB + ch + 1],
                                op0=ALU.mult, op1=ALU.add), reads=[uB], writes=[acB])
                            for j in range(1, 4):
                                S.op("dve", lambda e, u=u, ac=ac, ch=ch, j=j: e.scalar_tensor_tensor(
                                    out=ac[:], in0=u[:, j:j + 512], scalar=pv[:, PV_CW + j * 8 + ch:PV_CW + j * 8 + ch + 1], in1=ac[:],
                                    op0=ALU.mult, op1=ALU.add), reads=[uB, acB], writes=[acB])
                            S.op("act", lambda e, ac=ac, sg=sg: e.activation(out=sg[:], in_=ac[:], func=AF.Silu), reads=[acB], writes=[sgB])
                            if gi == 0:
                                if own:
                                    S.dma("sp", lambda e, sg=sg, c=c, ot0=ot0: e.dma_start(out=mqT[c, :, ot0:ot0 + 512], in_=sg[:]), reads=[sgB], writes=[mqB])
                            else:
                                S.dma("sp", lambda e, sg=sg, c=c, lt0=lt0: e.dma_start(out=mkT[c, :, lt0:lt0 + 512], in_=sg[:]), reads=[sgB], writes=[mkB])

            for q in range(4):
                S.dma("pool", lambda e, q=q: e.dma_start(out=wg[:, 4 * q:4 * q + 4, 0:8],
                                                         in_=w_in[q * 512:(q + 1) * 512, C_FF:C_FF + 8].rearrange("(kc p) c -> p kc c", p=P)), writes=[wgB])
                S.dma("pool", lambda e, q=q: e.dma_start(out=wg[:, 4 * q:4 * q + 4, 8:16],
                                                         in_=w_in[q * 512:(q + 1) * 512, C_MI:C_MI + 8].rearrange("(kc p) c -> p kc c", p=P)), writes=[wgB])
            for t in range(4):
                for gi, (c0, m) in enumerate(((0, 8), (8, 4), (12, 4))):
                    ps, psB = self.next_pbank()
                    for kc in range(16):
                        S.op("pe", lambda e, kc=kc, ps=ps, c0=c0, m=m, t=t: e.matmul(
                            ps[0:m, :], lhsT=wg[:, kc, c0:c0 + m], rhs=XN[:, kc, t * 512:(t + 1) * 512], start=(kc == 0), stop=(kc == 15)),
                            reads=[wgB, XNB[2 * t][kc], XNB[2 * t + 1][kc]], writes=[psB])
                    g_, g_B = gst[gi % 2], gstB[gi % 2]
                    S.op("act", lambda e, ps=ps, m=m, g_=g_: e.activation(out=g_[0:m, :], in_=ps[0:m, :], func=AF.Copy), reads=[psB], writes=[g_B])
                    S.dma("sp", lambda e, gi=gi, m=m, g_=g_, t=t: e.dma_start(out=gT[gi, 0:m, tok0 + t * 512:tok0 + (t + 1) * 512], in_=g_[0:m, :]),
                          reads=[g_B], writes=[gTB])


        inproj_pass(0, False)
        inproj_pass(2048, True)
        S.barrier()
        if self.stop_after == "A":
            return self.finish(outT)

        offB = PERS
        gt = [self.view(offB + i * 16384, [8, TL], F32) for i in range(7)]
        gtB = [Buf() for _ in range(7)]
        offB += 7 * 16384
        tabs = [self.view(offB + i * 2048, [P, 32, 16], F32) for i in range(2)]
        tabsB = [Buf() for _ in range(2)]
        offB += 2 * 2048
        ngT = self.view(offB, [P, 32, 8], F32)
        offB += 1024
        uT = self.view(offB, [P, 32, 4], F32)
        offB += 512
        crefB_ = self.view(offB, [P, 8, 16], F32)
        offB += 512
        rhoB_ = self.view(offB, [P, 4, 16], F32)
        offB += 256
        rhoR = self.view(offB, [4, 16], F32)
        offB += 64
        assert offB <= self.arena_bytes, offB
        ngTB, uTB, crefBB, rhoBB, rhoRB = Buf(), Buf(), Buf(), Buf(), Buf()
        A0, A1, A2, M0, M1, M2, M3 = range(7)
        T1, T2, T3, T4, T5, T6 = A1, A2, M0, M1, M2, M3
        for i, gi_, m_ in ((A0, 0, 8), (M0, 1, 4), (M1, 2, 4)):
            S.dma("sp", lambda e, i=i, gi_=gi_, m_=m_: e.dma_start(out=gt[i][0:m_, :], in_=gT[gi_, 0:m_, :]), reads=[gTB], writes=[gtB[i]])

        def softplus_neg(t, nrow, bcol):
            S.op("dve", lambda e: e.tensor_scalar(out=gt[t][0:nrow, :], in0=gt[t][0:nrow, :], scalar1=gb[0:nrow, bcol:bcol + 1], scalar2=None, op0=ALU.add),
                 reads=[gtB[t]], writes=[gtB[t]])
            S.op("act", lambda e: e.activation(out=gt[t][0:nrow, :], in_=gt[t][0:nrow, :], func=AF.Exp, scale=-1.0), reads=[gtB[t]], writes=[gtB[t]])
            S.op("act", lambda e: e.activation(out=gt[t][0:nrow, :], in_=gt[t][0:nrow, :], func=AF.Ln, bias=1.0, scale=1.0), reads=[gtB[t]], writes=[gtB[t]])
            S.op("dve", lambda e: e.tensor_scalar(out=gt[t][0:nrow, 0:2048], in0=gt[t][0:nrow, 0:2048], scalar1=pv[0:nrow, PV_FLAG:PV_FLAG + 1], scalar2=None, op0=ALU.mult),
                 reads=[gtB[t]], writes=[gtB[t]])

        def cumsum(dst, src, nrow):
            S.op("dve", lambda e: e.tensor_tensor_scan(out=gt[dst][0:nrow, :], data0=gt[src][0:nrow, :], data1=gt[src][0:nrow, :], initial=0.0,
                                                       op0=ALU.add, op1=ALU.max), reads=[gtB[src]], writes=[gtB[dst]])

        softplus_neg(A0, 8, 0)
        cumsum(T1, A0, 8)
        S.op("dve", lambda e: e.tensor_copy(out=gt[T2][:, :], in_=gt[T1][:, :]), reads=[gtB[T1]], writes=[gtB[T2]])
        S.op("dve", lambda e: e.tensor_scalar(out=gt[T2][:, 0:2048], in0=gt[T2][:, 0:2048], scalar1=pv[0:8, PV_PM:PV_PM + 1], scalar2=None, op0=ALU.add),
             reads=[gtB[T2]], writes=[gtB[T2]])
        S.op("dve", lambda e: e.tensor_scalar(out=gt[T3][0:4, :], in0=gt[T3][0:4, :], scalar1=gb[0:4, 1:2], scalar2=None, op0=ALU.add),
             reads=[gtB[T3]], writes=[gtB[T3]])
        S.op("dve", lambda e: e.tensor_scalar(out=gt[T3][0:4, 0:2048], in0=gt[T3][0:4, 0:2048], scalar1=pv[0:4, PV_PM:PV_PM + 1], scalar2=None, op0=ALU.add),
             reads=[gtB[T3]], writes=[gtB[T3]])
        softplus_neg(T4, 4, 2)
        cumsum(T5, T4, 4)
        S.op("dve", lambda e: e.tensor_tensor(out=gt[T3][0:4, :], in0=gt[T3][0:4, :], in1=gt[T5][0:4, :], op=ALU.add),
             reads=[gtB[T3], gtB[T5]], writes=[gtB[T3]])
        S.op("dve", lambda e: e.tensor_tensor_scan(out=gt[T6][0:4, :], data0=gt[T3][0:4, :], data1=gt[T3][0:4, :], initial=0.0, op0=ALU.max, op1=ALU.max),
             reads=[gtB[T3]], writes=[gtB[T6]])
        S.op("dve", lambda e: e.tensor_tensor(out=gt[T4][0:4, 2048:], in0=gt[T5][0:4, 2048:], in1=gt[T6][0:4, 2048:], op=ALU.subtract),
             reads=[gtB[T5], gtB[T6], gtB[T4]], writes=[gtB[T4]])
        S.op("act", lambda e: e.activation(out=gt[T4][0:4, 2048:], in_=gt[T4][0:4, 2048:], func=AF.Exp), reads=[gtB[T4]], writes=[gtB[T4]])
        S.dma("sp", lambda e: e.dma_start(out=lamem[:, 1, :], in_=gt[T4][0:4, 2048:]), reads=[gtB[T4]], writes=[lmB])
        S.op("dve", lambda e: e.tensor_copy(out=rhoR[:, :], in_=gt[T6][0:4, 2047:4095:128]), reads=[gtB[T6]], writes=[rhoRB])
        S.op("dve", lambda e: e.tensor_tensor(out=gt[A0][0:4, 0:2048].rearrange("r (j t) -> r j t", j=16),
                                               in0=rhoR[:, :].unsqueeze(2).to_broadcast([4, 16, 128]),
                                               in1=gt[T6][0:4, 2048:].rearrange("r (j t) -> r j t", j=16), op=ALU.subtract),
             reads=[rhoRB, gtB[T6], gtB[A0]], writes=[gtB[A0]])
        S.op("act", lambda e: e.activation(out=gt[A0][0:4, 0:2048], in_=gt[A0][0:4, 0:2048], func=AF.Exp), reads=[gtB[A0]], writes=[gtB[A0]])
        S.dma("sp", lambda e: e.dma_start(out=lamem[:, 0, :], in_=gt[A0][0:4, 0:2048]), reads=[gtB[A0]], writes=[lmB])
        tp, tpB = pb[6], pbB[6]
        tp2, tp2B = pb[7], pbB[7]
        for kb in range(32):
            S.op("pe", lambda e, kb=kb: e.transpose(tp[:, kb * 8:(kb + 1) * 8], gt[T2][0:8, kb * 128:(kb + 1) * 128], ident[0:8, 0:8]),
                 reads=[gtB[T2]], writes=[tpB])
            S.op("pe", lambda e, kb=kb: e.transpose(tp2[:, kb * 4:(kb + 1) * 4], gt[T3][0:4, kb * 128:(kb + 1) * 128], ident[0:4, 0:4]),
                 reads=[gtB[T3]], writes=[tp2B])
        S.op("dve", lambda e: e.tensor_copy(out=ngT[:].rearrange("p a b -> p (a b)"), in_=tp[:, 0:256]), reads=[tpB], writes=[ngTB])
        S.op("dve", lambda e: e.tensor_copy(out=uT[:].rearrange("p a b -> p (a b)"), in_=tp2[:, 0:128]), reads=[tp2B], writes=[uTB])
        bp, bpB = pb[5], pbB[5]
        for h in range(8):
            S.op("pe", lambda e, h=h: e.matmul(bp[:, h * 16:(h + 1) * 16], lhsT=sel[0:8, h, :], rhs=gt[T1][0:8, 2048 + 63:4096:128], start=True, stop=True),
                 reads=[gtB[T1]], writes=[bpB])
        for h in range(4):
            S.op("pe", lambda e, h=h: e.matmul(bp[:, 128 + h * 16:128 + (h + 1) * 16], lhsT=sel[0:4, h, :], rhs=rhoR[0:4, :], start=True, stop=True),
                 reads=[rhoRB], writes=[bpB])
        S.op("dve", lambda e: e.tensor_copy(out=crefB_[:].rearrange("p a b -> p (a b)"), in_=bp[:, 0:128]), reads=[bpB], writes=[crefBB])
        S.op("dve", lambda e: e.tensor_copy(out=rhoB_[:].rearrange("p a b -> p (a b)"), in_=bp[:, 128:192]), reads=[bpB], writes=[rhoBB])
        for h in range(8):
            tb, tbB = tabs[h % 2], tabsB[h % 2]
            S.op("dve", lambda e, h=h, tb=tb: e.tensor_tensor(out=tb[:], in0=ngT[:, :, h:h + 1].to_broadcast([P, 32, 16]),
                                                             in1=crefB_[:, h:h + 1, :].to_broadcast([P, 32, 16]), op=ALU.subtract),
                 reads=[ngTB, crefBB], writes=[tbB])
            S.op("dve", lambda e, tb=tb: e.tensor_scalar(out=tb[:], in0=tb[:], scalar1=60.0, scalar2=None, op0=ALU.min), reads=[tbB], writes=[tbB])
            S.op("act", lambda e, tb=tb: e.activation(out=tb[:], in_=tb[:], func=AF.Exp), reads=[tbB], writes=[tbB])
            S.dma("sp", lambda e, h=h, tb=tb: e.dma_start(out=Etab[h, :, :], in_=tb[:].rearrange("p a b -> p (a b)")), reads=[tbB], writes=[EtB])
        for h in range(4):
            tb, tbB = tabs[h % 2], tabsB[h % 2]
            S.op("dve", lambda e, h=h, tb=tb: e.tensor_tensor(out=tb[:], in0=uT[:, :, h:h + 1].to_broadcast([P, 32, 16]),
                                                             in1=rhoB_[:, h:h + 1, :].to_broadcast([P, 32, 16]), op=ALU.subtract),
                 reads=[uTB, rhoBB], writes=[tbB])
            S.op("dve", lambda e, tb=tb: e.tensor_scalar(out=tb[:], in0=tb[:], scalar1=60.0, scalar2=None, op0=ALU.min), reads=[tbB], writes=[tbB])
            S.op("act", lambda e, tb=tb: e.activation(out=tb[:], in_=tb[:], func=AF.Exp, bias=LN_SC), reads=[tbB], writes=[tbB])
            S.dma("sp", lambda e, h=h, tb=tb: e.dma_start(out=Wtab[h, :, :], in_=tb[:].rearrange("p a b -> p (a b)")), reads=[tbB], writes=[WtB])
        S.barrier()
        if self.stop_after == "B":
            return self.finish(outT)
        self.nbase = 6
        offC = PERS
        RES = self.view(offC, [P, 16, 1024], F32)
        slot_off = [offC, offC + 32768]
        offC += 65536
        bufA = self.view(offC, [P, 16, 1024], BF16)
        offC += 32768
        bufB = self.view(offC, [P, 16, 1024], BF16)
        offC += 32768
        assert offC <= self.arena_bytes, offC
        resB = [[Buf() for _ in range(2)] for _ in range(16)]
        bAB = [[Buf() for _ in range(2)] for _ in range(16)]
        bBB = [[Buf() for _ in range(2)] for _ in range(16)]
        ex = [self.stage[0], self.stage[1], xb16[0]]
        exB = [self.stageB[0], self.stageB[1], Buf()]
        pts = [self.stage[2], self.stage[3], self.sqh[0], xb16[1]]
        ptsB = [self.stageB[2], self.stageB[3], self.sqhB[0], Buf()]

        def slot_views(sl):
            o = slot_off[sl]
            Q = self.view(o, [P, 1024], BF16)
            K = self.view(o + 2048, [P, 4096], BF16)
            V = self.view(o + 10240, [P, 32, 256], BF16)
            Vf = self.view(o + 10240, [P, 32, 128], BF16)
            tab = self.view(o + 26624, [P, 32, 16], F32)
            lmr = self.view(o + 28672, [2, 1024], F32)
            return Q, K, V, Vf, tab, lmr
        slotB = [Buf(), Buf()]

        def linear_T(w, r0, nkc, col0s, src, srcB, evac):
            for cg, c0 in enumerate(col0s):
                wt, wb = self.load_w(w, r0, nkc, c0, 512)
                for t in range(2):
                    for c in range(4):
                        ps, psB = self.next_pbank()
                        for kc in range(nkc):
                            S.op("pe", lambda e, kc=kc, ps=ps, wt=wt, c=c, t=t: e.matmul(
                                ps[:, :], lhsT=wt[:, kc, c * 128:(c + 1) * 128], rhs=src[:, kc, t * 512:(t + 1) * 512], start=(kc == 0), stop=(kc == nkc - 1)),
                                reads=[wb, srcB[kc][t]], writes=[psB])
                        evac(cg, c, t, ps, psB)

        def evac_res_add(cg, c, t, ps, psB):
            kc = 4 * cg + c
            S.op("dve", lambda e, kc=kc, t=t, ps=ps: e.tensor_tensor(out=RES[:, kc, t * 512:(t + 1) * 512], in0=ps[:, :], in1=RES[:, kc, t * 512:(t + 1) * 512], op=ALU.add),
                 reads=[psB, resB[kc][t]], writes=[resB[kc][t]])

        def rms_res(gcol, dst, dstB):
            for t in range(2):
                self.rms_T(RES[:, :, t * 512:(t + 1) * 512], lambda kc, t=t: [resB[kc][t]], 512, gcol,
                           lambda kc, t=t: dst[:, kc, t * 512:(t + 1) * 512], lambda kc, t=t: [dstB[kc][t]], 1.0 / D)

        mx = self.view(PERS, [P, 16, 256], F32)
        mxB = Buf()
        mn = bufB[:, :, 0:256]
        mnB = [Buf() for _ in range(16)]
        for q in range(4):
            S.dma("sp", lambda e, q=q: e.dma_start(out=mx[:, q * 4:(q + 1) * 4, :], in_=memT[q * 512:(q + 1) * 512, :].rearrange("(kc p) t -> p kc t", p=P)), writes=[mxB])
        self.rms_T(mx[:, :, :], lambda kc: [mxB], 256, PV_MEM, lambda kc: bufB[:, kc, 0:256], lambda kc: [mnB[kc]], 1.0 / D)
        for hq in range(4):
            wt, wb = self.load_w(w_xkv, 0, 16, hq * 512, 512)
            pss = []
            for c in range(4):
                ps, psB = self.next_pbank()
                pss.append((ps, psB))
                for kc in range(16):
                    S.op("pe", lambda e, kc=kc, ps=ps, wt=wt, c=c: e.matmul(ps[:, 0:256], lhsT=wt[:, kc, c * 128:(c + 1) * 128], rhs=bufB[:, kc, 0:256],
                                                                           start=(kc == 0), stop=(kc == 15)), reads=[wb, mnB[kc]], writes=[psB])
            pbn, pbnB = self.next_nbank()
            for c in range(4):
                ps, psB = pss[c]
                sg, sgB = self.stage[c], self.stageB[c]
                S.op("act", lambda e, ps=ps, sg=sg: e.activation(out=sg[:, 0:256], in_=ps[:, 0:256], func=AF.Square), reads=[psB], writes=[sgB])
                S.op("pe", lambda e, sg=sg, c=c: e.matmul(pbn[:, 0:256], lhsT=self.ones[:], rhs=sg[:, 0:256], start=(c == 0), stop=(c == 3)), reads=[sgB], writes=[pbnB])
            S.op("act", lambda e: e.activation(out=tmpf[2][:, 0:256], in_=pbn[:, 0:256], func=AF.Ln, bias=self.epsc[:, 0:1], scale=1.0 / 512), reads=[pbnB], writes=[tmpfB[2]])
            S.op("act", lambda e: e.activation(out=tmpf[3][:, 0:256], in_=tmpf[2][:, 0:256], func=AF.Exp, scale=-0.5), reads=[tmpfB[2]], writes=[tmpfB[3]])
            for c in range(4):
                ps, psB = pss[c]
                S.op("dve", lambda e, ps=ps, c=c, hq=hq: e.scalar_tensor_tensor(out=kmem[:, 4 * hq + c, :], in0=ps[:, 0:256], scalar=pv[:, PV_XK + c:PV_XK + c + 1],
                                                                               in1=tmpf[3][:, 0:256], op0=ALU.mult, op1=ALU.mult),
                     reads=[psB, tmpfB[3]], writes=[kmemB[hq]])
        for gv in range(4):
            wt, wb = self.load_w(w_xkv, 0, 16, 2048 + gv * 512, 512)
            for mb in range(2):
                ps, psB = self.next_pbank()
                for kc in range(16):
                    S.op("pe", lambda e, kc=kc, ps=ps, wt=wt, mb=mb: e.matmul(ps[:, :], lhsT=bufB[:, kc, mb * 128:(mb + 1) * 128], rhs=wt[:, kc, :],
                                                                             start=(kc == 0), stop=(kc == 15)), reads=[wb, mnB[kc]], writes=[psB])
                S.op("act", lambda e, ps=ps, mb=mb, gv=gv: e.activation(out=vmem[:, mb, gv * 512:(gv + 1) * 512], in_=ps[:, :], func=AF.Copy), reads=[psB], writes=[vmemB])
        S.barrier()

        an = [0, False]
        FS = DeferS(S)

        def do_half(hf):
            o0 = hf * 1024
            self.nbase, self.nmod = 7, 1
            def load_head(hd):
                fox = hd < 8
                hm = hd - 8
                sl = an[0] % 2
                an[0] += 1
                Q, K, V, Vf, tab, lmr = slot_views(sl)
                sB = slotB[sl]
                nk = 16 + 8 * (hf + 1)
                if fox:
                    S.dma("sp", lambda e, Q=Q, hd=hd: e.dma_start(out=Q[:, :], in_=qfT[hd, :, o0:o0 + 1024]), reads=[qfB], writes=[sB])
                    S.dma("sp", lambda e, K=K, hd=hd: e.dma_start(out=K[:, 0:nk * 128], in_=kfT[hd, :, 0:nk * 128]), reads=[kfB], writes=[sB])
                    S.dma("sp", lambda e, Vf=Vf, hd=hd: e.dma_start(out=Vf[:, 0:nk, :], in_=vf[hd, :, 0:nk, :]), reads=[vfB], writes=[sB])
                    S.dma("sp", lambda e, tab=tab, hd=hd: e.dma_start(out=tab[:].rearrange("p a b -> p (a b)"), in_=Etab[hd, :, :]), reads=[EtB], writes=[sB])
                else:
                    S.dma("sp", lambda e, Q=Q, hm=hm: e.dma_start(out=Q[:, :], in_=mqT[hm, :, o0:o0 + 1024]), reads=[mqB], writes=[sB])
                    S.dma("sp", lambda e, K=K, hm=hm: e.dma_start(out=K[:, 0:nk * 128], in_=mkT[hm, :, 0:nk * 128]), reads=[mkB], writes=[sB])
                    S.dma("sp", lambda e, V=V, hm=hm: e.dma_start(out=V[:, 0:nk, :], in_=mvv[hm, :, 0:nk, :]), reads=[mvB], writes=[sB])
                    S.dma("sp", lambda e, tab=tab, hm=hm: e.dma_start(out=tab[:].rearrange("p a b -> p (a b)"), in_=Wtab[hm, :, :]), reads=[WtB], writes=[sB])
                    S.dma("sp", lambda e, lmr=lmr, hm=hm: e.dma_start(out=lmr[:, :], in_=lamem[hm, :, o0:o0 + 1024]), reads=[lmB], writes=[sB])
                return (hd, hm, fox, Q, K, V, Vf, tab, lmr, sB)

            def attn_group(gl, hd, hm, fox, Q, K, V, Vf, tab, lmr, sB):
                g = 2 * hf + gl
                nkb = 16 + 4 * (g + 1)
                it = an[0] * 2 + gl
                O0, O0B = pb[2], pbB[2]
                O1, O1B = pb[3], pbB[3]
                Dn, DnB = pb[4], pbB[4]
                sring = (0, 1, 6, 5, 3) if fox else (0, 1, 6, 5)
                def emit_S(kb):
                    jd = kb - (16 + 4 * g)
                    c0 = max(0, jd) * 128
                    busy_ = self.srecent[-3:]
                    for t_ in range(len(sring)):
                        bi_ = sring[(self.pbn + t_) % len(sring)]
                        if bi_ not in busy_:
                            break
                    self.pbn += 1
                    self.srecent.append(bi_)
                    psS, psSB = pb[bi_], pbB[bi_]
                    S.op("pe", lambda e, psS=psS, kb=kb, c0=c0: e.matmul(
                        psS[:, c0:512], lhsT=K[:, kb * 128:(kb + 1) * 128], rhs=Q[:, gl * 512 + c0:(gl + 1) * 512], start=True, stop=True),
                        reads=[sB], writes=[psSB])
                    return psS, psSB

                def emit_POD(kb, psS, psSB):
                    jd = kb - (16 + 4 * g)
                    c0 = max(0, jd) * 128
                    nb_ = (512 - c0) // 128
                    pt, ptB = pts[self.stn % 4], ptsB[self.stn % 4]
                    self.stn += 1
                    jt0 = 4 * g + c0 // 128
                    if fox:
                        exi, exiB = ex[kb % 3], exB[kb % 3]
                        S.op("act", lambda e: e.activation(out=exi[:, c0:512], in_=psS[:, c0:512], func=AF.Exp, scale=128.0 ** -0.5),
                             reads=[psSB], writes=[exiB])
                        src, srcB_ = exi, exiB
                    else:
                        src, srcB_ = psS, psSB
                    cr = c0
                    nr = nb_
                    if jd >= 0:
                        S.op("dve", lambda e, jt0=jt0: e.scalar_tensor_tensor(out=pt[:, c0:c0 + 128], in0=src[:, c0:c0 + 128], scalar=tab[:, kb, jt0:jt0 + 1],
                                                                     in1=tri[:, :], op0=ALU.mult, op1=ALU.mult),
                             reads=[srcB_, sB], writes=[ptB])
                        cr, nr, jt0 = c0 + 128, nb_ - 1, jt0 + 1
                    if nr > 0:
                        tsl = tab[:, kb, jt0:jt0 + nr].unsqueeze(2).to_broadcast([P, nr, 128])
                        meng = "pool" if (fox and jd < 0 and kb % 4 == 3) else "dve"
                        S.op(meng, lambda e: e.tensor_tensor(
                            out=pt[:, cr:512].rearrange("p (a b) -> p a b", a=nr), in0=src[:, cr:512].rearrange("p (a b) -> p a b", a=nr), in1=tsl, op=ALU.mult),
                            reads=[srcB_, sB], writes=[ptB], nowaw=(jd >= 0))
                    first, last = (kb == 0), (kb == nkb - 1)
                    if fox:
                        S.op("pe", lambda e: e.matmul(O0[:, c0:512], lhsT=Vf[:, kb, :], rhs=pt[:, c0:512], start=first, stop=last), reads=[sB, ptB], writes=[O0B])
                    else:
                        S.op("pe", lambda e: e.matmul(O0[:, c0:512], lhsT=V[:, kb, 0:128], rhs=pt[:, c0:512], start=first, stop=last), reads=[sB, ptB], writes=[O0B])
                        S.op("pe", lambda e: e.matmul(O1[:, c0:512], lhsT=V[:, kb, 128:256], rhs=pt[:, c0:512], start=first, stop=last), reads=[sB, ptB], writes=[O1B])
                    S.op("pe", lambda e: e.matmul(Dn[:, c0:512], lhsT=self.ones[:], rhs=pt[:, c0:512], start=first, stop=last), reads=[ptB], writes=[DnB])

                def fin():
                    cols = slice(gl * 512, (gl + 1) * 512)
                    if fox:
                        S.op("act", lambda e: e.activation(out=tmpf[4][:], in_=Dn[:, :], func=AF.Ln), reads=[DnB], writes=[tmpfB[4]])
                        S.op("dve", lambda e: e.tensor_copy(out=tmpf[5][:], in_=O0[:, :]), reads=[O0B], writes=[tmpfB[5]])
                        FS.op("act", lambda e: e.activation(out=tmpf[4][:], in_=tmpf[4][:], func=AF.Exp, scale=-1.0), reads=[tmpfB[4]], writes=[tmpfB[4]])
                        FS.op("dve", lambda e: e.tensor_tensor(out=tmpf[5][:], in0=tmpf[5][:], in1=tmpf[4][:], op=ALU.mult), reads=[tmpfB[5], tmpfB[4]], writes=[tmpfB[5]])
                        self.headnorm(tmpf[5][:], tmpfB[5], 512, PV_FO + hd, bufA[:, hd, cols], bAB[hd][gl], sq=self.sqh[1], sqB=self.sqhB[1], S=FS)
                    else:
                        FS.drain()
                        S.op("act", lambda e: e.activation(out=tmpf[6][:], in_=Dn[:, :], func=AF.Copy), reads=[DnB], writes=[tmpfB[6]])
                        S.op("act", lambda e: e.activation(out=tmpf[7][:], in_=O0[:, :], func=AF.Copy), reads=[O0B], writes=[tmpfB[7]])
                        S.op("act", lambda e: e.activation(out=tmpf[8][:], in_=O1[:, :], func=AF.Copy), reads=[O1B], writes=[tmpfB[8]])
                        for r_, ti in ((0, 4), (1, 5)):
                            pbn, pbnB = self.next_nbank()
                            S.op("pe", lambda e, pbn=pbn, r_=r_, lmr=lmr, cols=cols: e.matmul(pbn[:, :], lhsT=sel[0:2, r_, :], rhs=lmr[0:2, cols], start=True, stop=True),
                                 reads=[sB], writes=[pbnB])
                            S.op("act", lambda e, pbn=pbn, ti=ti: e.activation(out=tmpf[ti][:], in_=pbn[:, :], func=AF.Copy), reads=[pbnB], writes=[tmpfB[ti]])
                        FS.op("dve", lambda e: e.tensor_tensor(out=tmpf[6][:], in0=tmpf[6][:], in1=tmpf[4][:], op=ALU.mult), reads=[tmpfB[6], tmpfB[4]], writes=[tmpfB[6]])
                        FS.op("act", lambda e: e.activation(out=tmpf[6][:], in_=tmpf[6][:], func=AF.Abs), reads=[tmpfB[6]], writes=[tmpfB[6]])
                        FS.op("dve", lambda e: e.tensor_tensor(out=tmpf[6][:], in0=tmpf[6][:], in1=tmpf[5][:], op=ALU.max), reads=[tmpfB[6], tmpfB[5]], writes=[tmpfB[6]])
                        FS.op("act", lambda e: e.activation(out=tmpf[6][:], in_=tmpf[6][:], func=AF.Ln), reads=[tmpfB[6]], writes=[tmpfB[6]])
                        FS.op("act", lambda e: e.activation(out=tmpf[6][:], in_=tmpf[6][:], func=AF.Exp, scale=-1.0), reads=[tmpfB[6]], writes=[tmpfB[6]])
                        FS.op("dve", lambda e: e.tensor_tensor(out=tmpf[6][:], in0=tmpf[6][:], in1=tmpf[4][:], op=ALU.mult), reads=[tmpfB[6], tmpfB[4]], writes=[tmpfB[6]])
                        pbn, pbnB = self.next_nbank()
                        for c in range(2):
                            FS.op("dve", lambda e, c=c: e.tensor_tensor(out=tmpf[7 + c][:], in0=tmpf[7 + c][:], in1=tmpf[6][:], op=ALU.mult),
                                 reads=[tmpfB[7 + c], tmpfB[6]], writes=[tmpfB[7 + c]])
                            sq_, sq_B = (self.sqh[1], self.sqhB[1]) if c == 0 else (ex[0], exB[0])
                            FS.op("act", lambda e, c=c, sq_=sq_: e.activation(out=sq_[:], in_=tmpf[7 + c][:], func=AF.Square), reads=[tmpfB[7 + c]], writes=[sq_B])
                            FS.op("pe", lambda e, c=c, sq_=sq_, pbn=pbn: e.matmul(pbn[:, :], lhsT=self.ones[:], rhs=sq_[:], start=(c == 0), stop=(c == 1)), reads=[sq_B], writes=[pbnB])
                        FS.op("act", lambda e, pbn=pbn: e.activation(out=tmpf[2][:], in_=pbn[:, :], func=AF.Ln, bias=self.epsc[:, 0:1], scale=1.0 / 256), reads=[pbnB], writes=[tmpfB[2]])
                        FS.op("act", lambda e: e.activation(out=tmpf[3][:], in_=tmpf[2][:], func=AF.Exp, scale=-0.5), reads=[tmpfB[2]], writes=[tmpfB[3]])
                        for c in range(2):
                            ch = 2 * hm + c
                            mo_, mo_B = ex[1], exB[1]
                            FS.dma("sp", lambda e, ch=ch, mo_=mo_, gl=gl: e.dma_start(out=mo_[:], in_=mosT[ch, :, o0 + gl * 512:o0 + (gl + 1) * 512]), reads=[mosB], writes=[mo_B])
                            FS.op("dve", lambda e, c=c, ch=ch: e.scalar_tensor_tensor(out=tmpf[7 + c][:], in0=tmpf[7 + c][:], scalar=pv[:, PV_MO + ch:PV_MO + ch + 1],
                                                                                 in1=tmpf[3][:], op0=ALU.mult, op1=ALU.mult), reads=[tmpfB[7 + c], tmpfB[3]], writes=[tmpfB[7 + c]])
                            FS.op("dve", lambda e, c=c, ch=ch, mo_=mo_, cols=cols: e.tensor_tensor(out=bufA[:, 8 + ch, cols], in0=tmpf[7 + c][:], in1=mo_[:], op=ALU.mult),
                                 reads=[tmpfB[7 + c], mo_B], writes=[bAB[8 + ch][gl]])

                return dict(nkb=nkb, S=emit_S, POD=emit_POD, fin=fin, fox=fox)

            LA = 4
            glist = [(hd_, gl_) for hd_ in range(12) for gl_ in range(2)]
            heads = {0: load_head(0)}
            objs = {}

            def get_group(i_):
                if i_ not in objs:
                    hd_, gl_ = glist[i_]
                    if hd_ not in heads:
                        heads[hd_] = load_head(hd_)
                    objs[i_] = attn_group(gl_, *heads[hd_])
                return objs[i_]

            stream = []
            for i_ in range(len(glist)):
                hd_, gl_ = glist[i_]
                g_ = 2 * hf + gl_
                for kb_ in range(16 + 4 * (g_ + 1)):
                    stream.append((i_, kb_))
            pend = {}
            prev_fox = [True]

            def s_emit(j_):
                i_, kb_ = stream[j_]
                G = get_group(i_)
                pend[j_] = G["S"](kb_)

            for j_ in range(min(LA, len(stream))):
                s_emit(j_)
            for j_ in range(len(stream)):
                i_, kb_ = stream[j_]
                G = get_group(i_)
                if kb_ == 0:
                    hd_, gl_ = glist[i_]
                    if gl_ == 0 and hd_ + 1 < 12 and (hd_ + 1) not in heads:
                        heads[hd_ + 1] = load_head(hd_ + 1)
                    nper = (len(FS.q) + max(1, G["nkb"] - 8) - 1) // max(1, G["nkb"] - 8)
                if kb_ == 0:
                    if (not G["fox"]) and prev_fox[0]:
                        FS.drain()
                    prev_fox[0] = G["fox"]
                psS, psSB = pend.pop(j_)
                G["POD"](kb_, psS, psSB)
                if j_ + LA < len(stream):
                    s_emit(j_ + LA)
                if kb_ >= 2:
                    FS.drain(nper)
                if kb_ == G["nkb"] - 1:
                    FS.drain()
                    G["fin"]()
            FS.drain()
            if catT is not None:
                for kc in range(16):
                    S.dma("sp", lambda e, kc=kc: e.dma_start(out=catT[kc * P:(kc + 1) * P, o0:o0 + 1024], in_=bufA[:, kc, :]), reads=[bAB[kc][0], bAB[kc][1]])
            S.barrier()
            self.nbase, self.nmod = 6, 2
            self.pring = 6
            for q in range(4):
                S.dma("sp", lambda e, q=q: e.dma_start(out=RES[:, q * 4:(q + 1) * 4, :],
                                                       in_=xT[q * 512:(q + 1) * 512, 2048 + o0:2048 + o0 + 1024].rearrange("(kc p) t -> p kc t", p=P)),
                      writes=[resB[kc][t] for kc in range(4 * q, 4 * q + 4) for t in range(2)])
            linear_T(w_out, 0, 16, [0, 512, 1024, 1536], bufA, bAB, evac_res_add)
            if x1T is not None:
                for kc in range(16):
                    S.dma("sp", lambda e, kc=kc: e.dma_start(out=x1T[kc * P:(kc + 1) * P, o0:o0 + 1024], in_=RES[:, kc, :]), reads=[resB[kc][0], resB[kc][1]])
            rms_res(PV_XAT, bufB, bBB)
            for hq in range(4):
                wt, wb = self.load_w(w_xq, 0, 16, hq * 512, 512)
                for t in range(2):
                    pss = []
                    for c in range(4):
                        ps, psB = self.next_pbank()
                        pss.append((ps, psB))
                        for kc in range(16):
                            S.op("pe", lambda e, kc=kc, ps=ps, wt=wt, c=c, t=t: e.matmul(ps[:, :], lhsT=wt[:, kc, c * 128:(c + 1) * 128], rhs=bufB[:, kc, t * 512:(t + 1) * 512],
                                                                                   start=(kc == 0), stop=(kc == 15)), reads=[wb, bBB[kc][t]], writes=[psB])
                    pbn, pbnB = self.next_nbank()
                    for c in range(4):
                        ps, psB = pss[c]
                        sg, sgB = self.stage[c], self.stageB[c]
                        S.op("act", lambda e, ps=ps, sg=sg: e.activation(out=sg[:], in_=ps[:, :], func=AF.Square), reads=[psB], writes=[sgB])
                        S.op("pe", lambda e, sg=sg, c=c, pbn=pbn: e.matmul(pbn[:, :], lhsT=self.ones[:], rhs=sg[:], start=(c == 0), stop=(c == 3)), reads=[sgB], writes=[pbnB])
                    S.op("act", lambda e, pbn=pbn: e.activation(out=tmpf[2][:], in_=pbn[:, :], func=AF.Ln, bias=self.epsc[:, 0:1], scale=1.0 / 512), reads=[pbnB], writes=[tmpfB[2]])
                    S.op("act", lambda e: e.activation(out=tmpf[3][:], in_=tmpf[2][:], func=AF.Exp, scale=-0.5), reads=[tmpfB[2]], writes=[tmpfB[3]])
                    for c in range(4):
                        ps, psB = pss[c]
                        S.op("dve", lambda e, ps=ps, c=c, hq=hq, t=t: e.scalar_tensor_tensor(out=bufA[:, 4 * hq + c, t * 512:(t + 1) * 512], in0=ps[:, :],
                                                                                         scalar=pv[:, PV_XQ + c:PV_XQ + c + 1], in1=tmpf[3][:], op0=ALU.mult, op1=ALU.mult),
                             reads=[psB, tmpfB[3]], writes=[bAB[4 * hq + c][t]])
            for hq in range(4):
                for t in range(2):
                    pms = []
                    for mb in range(2):
                        psS, psSB = self.next_pbank()
                        for c in range(4):
                            S.op("pe", lambda e, psS=psS, c=c, hq=hq, mb=mb, t=t: e.matmul(psS[:, :], lhsT=kmem[:, 4 * hq + c, mb * 128:(mb + 1) * 128],
                                                                                      rhs=bufA[:, 4 * hq + c, t * 512:(t + 1) * 512], start=(c == 0), stop=(c == 3)),
                                 reads=[kmemB[hq], bAB[4 * hq + c][t]], writes=[psSB])
                        pm_, pm_B = pts[mb], ptsB[mb]
                        S.op("act", lambda e, psS=psS, pm_=pm_: e.activation(out=pm_[:], in_=psS[:, :], func=AF.Exp, scale=512.0 ** -0.5), reads=[psSB], writes=[pm_B])
                        pms.append((pm_, pm_B))
                    pbn, pbnB = self.next_nbank()
                    for mb in range(2):
                        S.op("pe", lambda e, mb=mb, pbn=pbn, pms=pms: e.matmul(pbn[:, :], lhsT=self.ones[:], rhs=pms[mb][0][:], start=(mb == 0), stop=(mb == 1)),
                             reads=[pms[mb][1]], writes=[pbnB])
                    S.op("act", lambda e, pbn=pbn: e.activation(out=tmpf[4][:], in_=pbn[:, :], func=AF.Ln), reads=[pbnB], writes=[tmpfB[4]])
                    S.op("act", lambda e: e.activation(out=tmpf[4][:], in_=tmpf[4][:], func=AF.Exp, scale=-1.0), reads=[tmpfB[4]], writes=[tmpfB[4]])
                    for c in range(4):
                        ps, psB = self.next_pbank()
                        for mb in range(2):
                            S.op("pe", lambda e, ps=ps, mb=mb, c=c, hq=hq, pms=pms: e.matmul(ps[:, :], lhsT=vmem[:, mb, hq * 512 + c * 128:hq * 512 + (c + 1) * 128],
                                                                                        rhs=pms[mb][0][:], start=(mb == 0), stop=(mb == 1)),
                                 reads=[vmemB, pms[mb][1]], writes=[psB])
                        S.op("dve", lambda e, ps=ps, c=c, hq=hq, t=t: e.tensor_tensor(out=bufB[:, 4 * hq + c, t * 512:(t + 1) * 512], in0=ps[:, :], in1=tmpf[4][:], op=ALU.mult),
                             reads=[psB, tmpfB[4]], writes=[bBB[4 * hq + c][t]])
            linear_T(w_xo, 0, 16, [0, 512, 1024, 1536], bufB, bBB, evac_res_add)
            if x2T is not None:
                for kc in range(16):
                    S.dma("sp", lambda e, kc=kc: e.dma_start(out=x2T[kc * P:(kc + 1) * P, o0:o0 + 1024], in_=RES[:, kc, :]), reads=[resB[kc][0], resB[kc][1]])
            rms_res(PV_MLP, bufA, bAB)
            for qf in range(4):
                def evac_up(cg, c, t, ps, psB):
                    kc = 4 * cg + c
                    r_, r_B = tmpf[4 + (kc % 2)], tmpfB[4 + (kc % 2)]
                    S.op("act", lambda e, ps=ps, r_=r_: e.activation(out=r_[:], in_=ps[:, :], func=AF.Relu), reads=[psB], writes=[r_B])
                    S.op("dve", lambda e, r_=r_, kc=kc, t=t: e.tensor_tensor(out=bufB[:, kc, t * 512:(t + 1) * 512], in0=r_[:], in1=r_[:], op=ALU.mult),
                         reads=[r_B], writes=[bBB[kc][t]])
                linear_T(w_up, 0, 16, [qf * 2048 + i * 512 for i in range(4)], bufA, bAB, evac_up)
                linear_T(w_down, qf * 2048, 16, [0, 512, 1024, 1536], bufB, bBB, evac_res_add)
            for q in range(4):
                S.dma("sp", lambda e, q=q: e.dma_start(out=outT[q * 512:(q + 1) * 512, o0:o0 + 1024].rearrange("(kc p) t -> p kc t", p=P), in_=RES[:, q * 4:(q + 1) * 4, :]),
                      reads=[resB[kc][t] for kc in range(4 * q, 4 * q + 4) for t in range(2)])
            S.barrier()

        for hf_ in range(2):
            do_half(hf_)
        self.stop_after = None
        return self.finish(outT)

    def finish(self, outT):
        S = self.S
        if self.stop_after is not None:
            S.dma("sp", lambda e: e.dma_start(out=outT[0:P, 0:512], in_=self.tmpf[0][:]))
        self.S.emit(self.nc, self.st)
        self.st.close()
        return self.nc


def _prep_inputs(inputs):
    g = {k: np.asarray(v) for k, v in inputs.items()}
    x = g["x"]
    f32 = np.float32

    def col16(v):
        return np.ascontiguousarray(v.reshape(16, 128).T)

    pv = np.zeros((P, NPV), f32)
    pv[:, PV_MIX:PV_MIX + 16] = col16(g["mixer_norm"][0])
    pv[:, PV_XAT:PV_XAT + 16] = col16(g["xattn_norm"][0])
    pv[:, PV_MEM:PV_MEM + 16] = col16(g["mem_norm"][0])
    pv[:, PV_MLP:PV_MLP + 16] = col16(g["mlp_norm"][0])
    pv[:, PV_FQ] = g["fox_q_norm"][0]
    pv[:, PV_FK] = g["fox_k_norm"][0]
    pv[:, PV_FO:PV_FO + 8] = g["fox_out_norm"][0].reshape(8, 128).T
    pv[:, PV_MO:PV_MO + 8] = g["mlstm_out_norm"][0].reshape(8, 128).T
    pv[:, PV_XQ:PV_XQ + 4] = g["xq_norm"][0].reshape(4, 128).T
    pv[:, PV_XK:PV_XK + 4] = g["xk_norm"][0].reshape(4, 128).T
    pv[:, PV_CW:PV_CW + 32] = g["conv_w"][0].reshape(4, 8, 128).transpose(2, 0, 1).reshape(128, 32)
    pv[:, PV_CB:PV_CB + 8] = g["conv_b"][0].reshape(8, 128).T
    gb = np.zeros((8, 4), f32)
    gb[:, 0] = g["fox_f_bias"][0]
    gb[0:4, 1] = g["mlstm_i_bias"][0]
    gb[0:4, 2] = g["mlstm_f_bias"][0]
    shared = {
        "w_in": np.ascontiguousarray(g["w_in"][0]), "w_out": np.ascontiguousarray(g["w_out"][0]),
        "w_xq": np.ascontiguousarray(g["w_xq"][0]), "w_xkv": np.ascontiguousarray(g["w_xkv"][0]),
        "w_xo": np.ascontiguousarray(g["w_xo"][0]), "w_up": np.ascontiguousarray(g["w_up"][0]),
        "w_down": np.ascontiguousarray(g["w_down"][0]), "gb": gb,
    }
    in_maps = []
    for c in range(8):
        b, h = c // 2, c % 2
        xT = np.zeros((D, TL), f32)
        if h == 1:
            xT[:, 0:2048] = x[b, 0:2048].T
        xT[:, 2048:] = x[b, h * 2048:(h + 1) * 2048].T
        pvc = pv.copy()
        pvc[:, PV_FLAG] = 1.0 if h == 1 else 0.0
        pvc[:, PV_PM] = 0.0 if h == 1 else PMASK
        m = dict(shared)
        m["xT"] = xT
        m["memT"] = np.ascontiguousarray(g["mem"][b].T)
        m["pv"] = pvc
        in_maps.append(m)
    return in_maps


_CACHE = {}


def kernel(**inputs):
    in_maps = _prep_inputs(inputs)
    if "nc" not in _CACHE:
        _CACHE["nc"] = Builder().build()
    res = run_bass_kernel_spmd(_CACHE["nc"], in_maps, core_ids=list(range(8)))
    out = np.zeros((4, 4096, D), np.float32)
    for c in range(8):
        b, h = c // 2, c % 2
        out[b, h * 2048:(h + 1) * 2048, :] = res.results[c]["outT"].T
    return out
```

```python
from contextlib import ExitStack
import numpy as np
import concourse.bass as bass
import concourse.mybir as mybir
from concourse.bass_utils import run_bass_kernel_spmd

F32 = mybir.dt.float32
BF16 = mybir.dt.bfloat16
AF = mybir.ActivationFunctionType
ALU = mybir.AluOpType

P = 128
D = 2048
KD = 16
TL = 4096
TO = 2048
NPV = 132
EPS = 1e-6
PMASK = -30000.0
LN_SC = -0.5 * float(np.log(128.0))

C_FQ, C_FK, C_FV, C_FF, C_MQK, C_MV, C_MI, C_MF, C_MO = 0, 1024, 2048, 3072, 3080, 4104, 5128, 5132, 5136
PV_MIX, PV_XAT, PV_MEM, PV_MLP, PV_FQ, PV_FK, PV_FO, PV_MO, PV_XQ, PV_XK, PV_CW, PV_CB, PV_FLAG, PV_PM = (
    0, 16, 32, 48, 64, 65, 66, 74, 82, 86, 90, 122, 130, 131)


class Buf:
    __slots__ = ("name", "w", "rs")

    def __init__(self, name=""):
        self.name = name
        self.w = None
        self.rs = []


class Ins:
    __slots__ = ("eng", "fn", "deps", "marked", "idx", "is_dma", "slot", "val", "n")

    def __init__(self, eng, fn, is_dma=False):
        self.eng = eng
        self.fn = fn
        self.deps = []
        self.marked = False
        self.idx = 0
        self.is_dma = is_dma
        self.slot = 0
        self.val = 0
        self.n = 0


class Sched:
    ENGS = ("pe", "act", "dve", "pool", "sp")
    NSLOT = 12

    def __init__(self):
        self.q = {e: [] for e in self.ENGS}
        self.ndma = {e: 0 for e in self.ENGS}
        self.all_dma = []
        self.pending = {e: [] for e in self.ENGS}
        self.last = {e: None for e in self.ENGS}
        self.lastdma = {}

    def _track(self, ins, reads, writes, nowaw=False):
        deps = ins.deps
        for b in reads:
            if b.w is not None:
                deps.append(b.w)
        for b in writes:
            if b.w is not None and not (nowaw and (not b.w.is_dma) and b.w.eng == ins.eng):
                deps.append(b.w)
            deps.extend(b.rs)
        for b in reads:
            rs = b.rs
            if not ins.is_dma:
                for i_ in range(len(rs)):
                    if (not rs[i_].is_dma) and rs[i_].eng == ins.eng:
                        rs[i_] = ins
                        break
                else:
                    rs.append(ins)
            else:
                rs.append(ins)
        for b in writes:
            b.w = ins
            b.rs = []

    def barrier(self):
        deps = [v for v in self.last.values() if v is not None] + list(self.lastdma.values())
        for e in self.ENGS:
            self.pending[e] = list(deps)

    def _pend(self, ins):
        if self.pending[ins.eng]:
            ins.deps.extend(self.pending[ins.eng])
            self.pending[ins.eng] = []
        if ins.is_dma:
            self.lastdma[(ins.eng, ins.slot)] = ins
        else:
            self.last[ins.eng] = ins

    def op(self, eng, fn, reads=(), writes=(), nowaw=False):
        ins = Ins(eng, fn)
        self._track(ins, reads, writes, nowaw)
        self._pend(ins)
        self.q[eng].append(ins)
        return ins

    def dma(self, eng, fn, reads=(), writes=()):
        ins = Ins(eng, fn, is_dma=True)
        self._track(ins, reads, writes)
        ins.n = self.ndma[eng]
        self.ndma[eng] += 1
        ins.slot = ins.n % self.NSLOT
        ins.val = 16 * (ins.n // self.NSLOT + 1)
        self._pend(ins)
        self.q[eng].append(ins)
        self.all_dma.append(ins)
        return ins

    def emit(self, nc, stack):
        for e in self.ENGS:
            for ins in self.q[e]:
                nd = []
                seen = set()
                for d in ins.deps:
                    if id(d) in seen:
                        continue
                    seen.add(id(d))
                    if not d.is_dma and d.eng == "pe" and ins.eng == "pe" and not ins.is_dma:
                        continue
                    nd.append(d)
                    if not d.is_dma:
                        d.marked = True
                ins.deps = nd
        for e in self.ENGS:
            c = 0
            for ins in self.q[e]:
                if not ins.is_dma and ins.marked:
                    c += 1
                    ins.idx = c
        esem = {e: stack.enter_context(nc.semaphore("s_" + e)) for e in self.ENGS}
        dsem = {}
        for e in self.ENGS:
            if self.ndma[e]:
                dsem[e] = [stack.enter_context(nc.semaphore("d_%s_%d" % (e, i))) for i in range(self.NSLOT)]
        block = stack.enter_context(nc.Block())
        sched = self

        def run(e, eng):
            waited = {}

            def wait(key, sem, val):
                if val <= 0:
                    return
                if waited.get(key, 0) >= val:
                    return
                waited[key] = val
                eng.wait_ge(sem, val)

            for ins in sched.q[e]:
                for d in ins.deps:
                    if d.is_dma:
                        wait((d.eng, d.slot), dsem[d.eng][d.slot], d.val)
                    else:
                        wait(d.eng, esem[d.eng], d.idx)
                if ins.is_dma:
                    wait((e, ins.slot), dsem[e][ins.slot], ins.val - 16)
                    ins.fn(eng).then_inc(dsem[e][ins.slot], 16)
                else:
                    r = ins.fn(eng)
                    if ins.marked:
                        r.then_inc(esem[e], 1)
            if e == "sp":
                last = {}
                for d in sched.all_dma:
                    last[(d.eng, d.slot)] = max(last.get((d.eng, d.slot), 0), d.val)
                for (de, sl), v in last.items():
                    wait((de, sl), dsem[de][sl], v)

        @block.tensor
        def _(eng):
            run("pe", eng)

        @block.scalar
        def _(eng):
            run("act", eng)

        @block.vector
        def _(eng):
            run("dve", eng)

        @block.gpsimd
        def _(eng):
            run("pool", eng)

        @block.sync
        def _(eng):
            run("sp", eng)


class DeferS:
    def __init__(self, S):
        self.S = S
        self.q = []

    def op(self, *a, **k):
        self.q.append(("op", a, k))

    def dma(self, *a, **k):
        self.q.append(("dma", a, k))

    def drain(self, n=None):
        n = len(self.q) if n is None else min(n, len(self.q))
        for _ in range(n):
            kind, a, k = self.q.pop(0)
            getattr(self.S, kind)(*a, **k)


class Builder:
    def __init__(self, stop_after=None, dbg=()):
        self.stop_after = stop_after
        self.dbg = set(dbg)
        self.nc = bass.Bass("TRN2", target_bir_lowering=False)
        self.S = Sched()
        self.wn = 0

    def dram(self, name, shape, dt, kind=None):
        if kind is None:
            kind = "ExternalOutput" if name in self.dbg else "Internal"
        return self.nc.dram_tensor(name, list(shape), dt, kind=kind).ap()

    def view(self, off, shape, dt):
        esz = 2 if dt == BF16 else 4
        n = 1
        for s in shape[1:]:
            n *= s
        nbytes = n * esz
        assert off % 4 == 0 and nbytes % 4 == 0
        assert off + nbytes <= self.arena_bytes, (off, nbytes)
        v = self.arena[:, off // 4:(off + nbytes) // 4]
        if dt == BF16:
            v = v.bitcast(BF16)
        if len(shape) == 3:
            v = v.rearrange("p (a b) -> p a b", a=shape[1])
        elif len(shape) == 4:
            v = v.rearrange("p (a b c) -> p a b c", a=shape[1], b=shape[2])
        if shape[0] < P:
            v = v[0:shape[0]]
        return v

    def load_w(self, src, r0, nkc, c0, ncols):
        S = self.S
        sl = self.wn % self.NW
        self.wn += 1
        wt = self.wring[sl]
        wb = self.wB[sl]
        step = 4 if ncols >= 256 else nkc
        for q in range(0, nkc, step):
            n = min(step, nkc - q)
            S.dma("pool", lambda e, q=q, n=n: e.dma_start(
                out=wt[:, q:q + n, 0:ncols],
                in_=src[r0 + q * P:r0 + (q + n) * P, c0:c0 + ncols].rearrange("(kc p) c -> p kc c", p=P)),
                writes=[wb])
        return wt, wb

    def rms_T(self, src3, srcB_fn, n, gcol, dst_fn, dstB_fn, scale):
        S = self.S
        KC = src3.shape[1]
        pbn, pbnB = self.next_nbank()
        lnb, lnbB = self.tmpf[0], self.tmpfB[0]
        rs, rsB = self.tmpf[1], self.tmpfB[1]
        for kc in range(KC):
            sqh, sqhB = self.sqh[self.sqn % 2], self.sqhB[self.sqn % 2]
            self.sqn += 1
            S.op("act", lambda e, kc=kc, sqh=sqh: e.activation(out=sqh[:, 0:n], in_=src3[:, kc, :], func=AF.Square), reads=srcB_fn(kc), writes=[sqhB])
            S.op("pe", lambda e, kc=kc, sqh=sqh: e.matmul(pbn[:, 0:n], lhsT=self.ones[:], rhs=sqh[:, 0:n], start=(kc == 0), stop=(kc == KC - 1)),
                 reads=[sqhB], writes=[pbnB])
        S.op("act", lambda e: e.activation(out=lnb[:, 0:n], in_=pbn[:, 0:n], func=AF.Ln, bias=self.epsc[:, 0:1], scale=scale), reads=[pbnB], writes=[lnbB])
        S.op("act", lambda e: e.activation(out=rs[:, 0:n], in_=lnb[:, 0:n], func=AF.Exp, scale=-0.5), reads=[lnbB], writes=[rsB])
        for kc in range(KC):
            S.op("dve", lambda e, kc=kc: e.scalar_tensor_tensor(out=dst_fn(kc), in0=src3[:, kc, :], scalar=self.pv[:, gcol + kc:gcol + kc + 1],
                                                                 in1=rs[:, 0:n], op0=ALU.mult, op1=ALU.mult),
                 reads=list(srcB_fn(kc)) + [rsB], writes=dstB_fn(kc))

    def next_nbank(self):
        i = self.nbase + (self.nb % self.nmod)
        self.nb += 1
        return self.pb[i], self.pbB[i]

    def next_pbank(self):
        i = self.pbn % self.pring
        self.pbn += 1
        return self.pb[i], self.pbB[i]

    def next_stage(self):
        i = self.stn % len(self.stage)
        self.stn += 1
        return self.stage[i], self.stageB[i]

    def headnorm(self, ps, psB, n, gcol, dst, dstB, scale=1.0 / 128, sq=None, sqB=None, S=None):
        S = self.S if S is None else S
        if sq is None:
            sqh, sqhB = self.sqh[self.sqn % 2], self.sqhB[self.sqn % 2]
            self.sqn += 1
        else:
            sqh, sqhB = sq, sqB
        pbn, pbnB = self.next_nbank()
        lnb, lnbB = self.tmpf[2], self.tmpfB[2]
        rs, rsB = self.tmpf[3], self.tmpfB[3]
        S.op("act", lambda e: e.activation(out=sqh[:, 0:n], in_=ps, func=AF.Square), reads=[psB], writes=[sqhB])
        S.op("pe", lambda e: e.matmul(pbn[:, 0:n], lhsT=self.ones[:], rhs=sqh[:, 0:n], start=True, stop=True), reads=[sqhB], writes=[pbnB])
        S.op("act", lambda e: e.activation(out=lnb[:, 0:n], in_=pbn[:, 0:n], func=AF.Ln, bias=self.epsc[:, 0:1], scale=scale), reads=[pbnB], writes=[lnbB])
        S.op("act", lambda e: e.activation(out=rs[:, 0:n], in_=lnb[:, 0:n], func=AF.Exp, scale=-0.5), reads=[lnbB], writes=[rsB])
        S.op("dve", lambda e: e.scalar_tensor_tensor(out=dst, in0=ps, scalar=self.pv[:, gcol:gcol + 1], in1=rs[:, 0:n], op0=ALU.mult, op1=ALU.mult),
             reads=[psB, rsB], writes=[dstB])

    def build(self):
        nc, S = self.nc, self.S
        inp = {}

        def din(name, shape):
            inp[name] = nc.dram_tensor(name, list(shape), F32, kind="ExternalInput").ap()
            return inp[name]

        xT = din("xT", [D, TL])
        memT = din("memT", [D, 256])
        w_in = din("w_in", [D, 6160])
        w_out = din("w_out", [D, D])
        w_xq = din("w_xq", [D, D])
        w_xkv = din("w_xkv", [D, 2 * D])
        w_xo = din("w_xo", [D, D])
        w_up = din("w_up", [D, 4 * D])
        w_down = din("w_down", [4 * D, D])
        pvd = din("pv", [P, NPV])
        gbd = din("gb", [8, 4])
        outT = nc.dram_tensor("outT", [D, TO], F32, kind="ExternalOutput").ap()

        qfT = self.dram("qfT", [8, P, TO], BF16)
        kfT = self.dram("kfT", [8, P, TL], BF16)
        vf = self.dram("vf", [8, P, 32, 128], BF16)
        mqT = self.dram("mqT", [4, P, TO], BF16)
        mkT = self.dram("mkT", [4, P, TL], BF16)
        mvv = self.dram("mvv", [4, P, 32, 256], BF16)
        mosT = self.dram("mosT", [8, P, TO], BF16)
        gT = self.dram("gT", [3, 8, TL], F32)
        Etab = self.dram("Etab", [8, P, 512], F32)
        Wtab = self.dram("Wtab", [4, P, 512], F32)
        lamem = self.dram("lamem", [4, 2, TO], F32)
        catT = self.dram("catT", [D, TO], BF16) if "catT" in self.dbg else None
        x1T = self.dram("x1T", [D, TO], F32) if "x1T" in self.dbg else None
        x2T = self.dram("x2T", [D, TO], F32) if "x2T" in self.dbg else None
        qfB, kfB, vfB, mqB, mkB, mvB, mosB, gTB, EtB, WtB, lmB = [Buf() for _ in range(11)]

        st = ExitStack()
        self.st = st
        self.arena_bytes = 206 * 1024
        self.arena = st.enter_context(nc.sbuf_tensor("arena", [P, self.arena_bytes // 4], F32))
        self.pb = [st.enter_context(nc.psum_tensor("pb%d" % i, [P, 512], F32)) for i in range(8)]
        self.pbB = [Buf("pb%d" % i) for i in range(8)]
        self.nb = 0
        self.nbase = 4
        self.nmod = 2
        self.srecent = []
        self.pring = 4
        self.pbn = 0
        self.stn = 0
        self.sqn = 0
        pb, pbB = self.pb, self.pbB

        off = 0

        def take(nbytes):
            nonlocal off
            o = off
            off += (nbytes + 31) // 32 * 32
            return o

        self.ones = self.view(take(256), [P, 128], BF16)
        ident = self.view(take(512), [P, 128], F32)
        tri = self.view(take(256), [P, 128], BF16)
        self.pv = self.view(take(NPV * 4), [P, NPV], F32)
        pv = self.pv
        self.epsc = self.view(take(32), [P, 8], F32)
        gb = self.view(take(16), [8, 4], F32)
        sel = self.view(take(8 * 128 * 4), [8, 8, 128], F32)
        halo = self.view(take(8 * 4 * 4), [P, 8, 4], F32)
        self.NW = 2
        self.wring = [self.view(take(16 * 512 * 2), [P, 16, 512], BF16) for _ in range(self.NW)]
        self.wB = [Buf("w%d" % i) for i in range(self.NW)]
        kmem = self.view(take(16 * 256 * 2), [P, 16, 256], BF16)
        vmem = self.view(take(2 * 2048 * 2), [P, 2, 2048], BF16)
        kmemB = [Buf() for _ in range(4)]
        vmemB = Buf()
        tmpf_off = [take(2048) for _ in range(9)]
        self.tmpf = [self.view(o_, [P, 512], F32) for o_ in tmpf_off]
        xb16 = [self.view(tmpf_off[i_], [P, 512], BF16) for i_ in range(2)]
        self.tmpfB = [Buf("tmpf%d" % i) for i in range(9)]
        self.sqh = [self.view(take(1024), [P, 512], BF16) for _ in range(2)]
        self.sqhB = [Buf() for _ in range(2)]
        self.stage = [self.view(take(1024), [P, 512], BF16) for _ in range(4)]
        self.stageB = [Buf() for _ in range(4)]
        PERS = off
        tmpf, tmpfB = self.tmpf, self.tmpfB

        cI, cT = Buf("ident"), Buf("tri")
        S.op("pool", lambda e: e.memset(self.ones[:], 1.0))
        S.op("pool", lambda e: e.memset(ident[:], 0.0), writes=[cI])
        S.op("pool", lambda e: e.affine_select(out=ident[:], in_=ident[:], pattern=[[-1, 128]], compare_op=ALU.not_equal, fill=1.0,
                                               base=0, channel_multiplier=1), reads=[cI], writes=[cI])
        S.op("pool", lambda e: e.memset(tri[:], 1.0), writes=[cT])
        S.op("pool", lambda e: e.affine_select(out=tri[:], in_=tri[:], pattern=[[1, 128]], compare_op=ALU.is_ge, fill=0.0,
                                               base=0, channel_multiplier=-1), reads=[cT], writes=[cT])
        S.op("pool", lambda e: e.memset(self.epsc[:], EPS))
        S.op("pool", lambda e: e.memset(halo[:], 0.0))
        S.dma("sp", lambda e: e.dma_start(out=pv[:], in_=pvd[:, :]))
        S.dma("sp", lambda e: e.dma_start(out=gb[:], in_=gbd[:, :]))
        S.barrier()
        S.op("dve", lambda e: e.tensor_copy(out=sel[:], in_=ident[0:8, 0:8].unsqueeze(2).to_broadcast([8, 8, 128])))
        S.barrier()

        offA = PERS
        XN = self.view(offA, [P, 16, 2048], BF16)
        offA += 16 * 2048 * 2
        XNB = [[Buf() for _ in range(16)] for _ in range(8)]
        xs = [self.view(offA + i * 16384, [P, 16, 256], F32) for i in range(2)]
        xsB = [Buf() for _ in range(2)]
        offA += 2 * 16384
        ub = [self.view(offA + i * 2064, [P, 516], F32) for i in range(2)]
        ubB = [Buf() for _ in range(2)]
        offA += 2 * 2064
        accb = [self.view(offA + i * 2048, [P, 512], F32) for i in range(2)]
        accB = [Buf() for _ in range(2)]
        offA += 2 * 2048
        gst = [self.view(offA + i * 2048, [8, 512], F32) for i in range(2)]
        gstB = [Buf() for _ in range(2)]
        offA += 2 * 2048
        wg = self.view(offA, [P, 16, 16], BF16)
        wgB = Buf("wg")
        offA += 512
        haloB = [Buf() for _ in range(8)]
        assert offA <= self.arena_bytes, offA

        def xn_tile_reads(t):
            return [XNB[2 * t + s][kc] for s in range(2) for kc in range(16)]

        def inproj_pass(tok0, own):
            for sub in range(8):
                xsl, xslB = xs[sub % 2], xsB[sub % 2]
                t0 = tok0 + sub * 256
                for q in range(4):
                    S.dma("sp", lambda e, q=q, xsl=xsl, t0=t0: e.dma_start(
                        out=xsl[:, q * 4:(q + 1) * 4, :],
                        in_=xT[q * 512:(q + 1) * 512, t0:t0 + 256].rearrange("(kc p) t -> p kc t", p=P)), writes=[xslB])
                self.rms_T(xsl[:, :, :], lambda kc, xslB=xslB: [xslB], 256, PV_MIX,
                           lambda kc, sub=sub: XN[:, kc, sub * 256:(sub + 1) * 256],
                           lambda kc, sub=sub: [XNB[sub][kc]], 1.0 / D)

            groups = []
            if own:
                groups += [("fq", 0), ("fq", 1)]
            groups += [("fk", 0), ("fk", 1), ("fv", 0), ("fv", 1), ("mqk", 0), ("mqk", 1), ("mv", 0), ("mv", 1)]
            if own:
                groups += [("mo", 0), ("mo", 1)]
            cbase = {"fq": C_FQ, "fk": C_FK, "fv": C_FV, "mqk": C_MQK, "mv": C_MV, "mo": C_MO}
            for kind, gi in groups:
                wt, wb = self.load_w(w_in, 0, 16, cbase[kind] + gi * 512, 512)
                tiles = range(4)
                if kind == "mqk" and gi == 0 and not own:
                    tiles = [3]
                for t in tiles:
                    lt0 = tok0 + t * 512
                    ot0 = t * 512
                    if kind in ("fv", "mv"):
                        for bl in range(4):
                            ps, psB = self.next_pbank()
                            c_lo = t * 512 + bl * 128
                            for kc in range(16):
                                S.op("pe", lambda e, kc=kc, ps=ps, wt=wt, c_lo=c_lo: e.matmul(
                                    ps[:, :], lhsT=XN[:, kc, c_lo:c_lo + 128], rhs=wt[:, kc, :], start=(kc == 0), stop=(kc == 15)),
                                    reads=[wb, XNB[c_lo // 256][kc]], writes=[psB])
                            sg, sgB = self.next_stage()
                            S.op("act", lambda e, ps=ps, sg=sg: e.activation(out=sg[:], in_=ps[:], func=AF.Copy), reads=[psB], writes=[sgB])
                            kb = (lt0 + bl * 128) // 128
                            if kind == "fv":
                                S.dma("sp", lambda e, sg=sg, kb=kb, gi=gi: e.dma_start(
                                    out=vf[4 * gi:4 * gi + 4, :, kb, :].rearrange("h p d -> p h d"),
                                    in_=sg[:].rearrange("p (h d) -> p h d", h=4)), reads=[sgB], writes=[vfB])
                            else:
                                S.dma("sp", lambda e, sg=sg, kb=kb, gi=gi: e.dma_start(
                                    out=mvv[2 * gi:2 * gi + 2, :, kb, :].rearrange("h p d -> p h d"),
                                    in_=sg[:].rearrange("p (h d) -> p h d", h=2)), reads=[sgB], writes=[mvB])
                        continue
                    for c in range(4):
                        ps, psB = self.next_pbank()
                        for kc in range(16):
                            S.op("pe", lambda e, kc=kc, ps=ps, wt=wt, c=c, t=t: e.matmul(
                                ps[:, :], lhsT=wt[:, kc, c * 128:(c + 1) * 128], rhs=XN[:, kc, t * 512:(t + 1) * 512], start=(kc == 0), stop=(kc == 15)),
                                reads=[wb, XNB[2 * t][kc], XNB[2 * t + 1][kc]], writes=[psB])
                        sg, sgB = self.next_stage()
                        hh = gi * 4 + c
                        if kind == "fq":
                            self.headnorm(ps[:, :], psB, 512, PV_FQ, sg[:], sgB)
                            S.dma("sp", lambda e, sg=sg, hh=hh, ot0=ot0: e.dma_start(out=qfT[hh, :, ot0:ot0 + 512], in_=sg[:]), reads=[sgB], writes=[qfB])
                        elif kind == "fk":
                            self.headnorm(ps[:, :], psB, 512, PV_FK, sg[:], sgB)
                            S.dma("sp", lambda e, sg=sg, hh=hh, lt0=lt0: e.dma_start(out=kfT[hh, :, lt0:lt0 + 512], in_=sg[:]), reads=[sgB], writes=[kfB])
                        elif kind == "mo":
                            S.op("act", lambda e, ps=ps, sg=sg: e.activation(out=sg[:], in_=ps[:], func=AF.Sigmoid), reads=[psB], writes=[sgB])
                            S.dma("sp", lambda e, sg=sg, hh=hh, ot0=ot0: e.dma_start(out=mosT[hh, :, ot0:ot0 + 512], in_=sg[:]), reads=[sgB], writes=[mosB])
                        else:
                            ch = gi * 4 + c
                            u, uB = ub[ch % 2], ubB[ch % 2]
                            ac, acB = accb[ch % 2], accB[ch % 2]
                            S.op("dve", lambda e, u=u, ch=ch: e.tensor_copy(out=u[:, 0:3], in_=halo[:, ch, 0:3]), reads=[haloB[ch]], writes=[uB])
                            S.op("act", lambda e, u=u, ps=ps: e.activation(out=u[:, 3:515], in_=ps[:], func=AF.Copy), reads=[psB], writes=[uB])
                            S.op("dve", lambda e, u=u, ch=ch: e.tensor_copy(out=halo[:, ch, 0:3], in_=u[:, 512:515]), reads=[uB], writes=[haloB[ch]])
                            S.op("dve", lambda e, u=u, ac=ac, ch=ch: e.tensor_scalar(
                                out=ac[:], in0=u[:, 0:512], scalar1=pv[:, PV_CW + ch:PV_CW + ch + 1], scalar2=pv[:, PV_CB + ch:PV_CB + ch + 1],
                                op0=ALU.mult, op1=ALU.add), reads=[uB], writes=[acB])
                            for j in range(1, 4):
                                S.op("dve", lambda e, u=u, ac=ac, ch=ch, j=j: e.scalar_tensor_tensor(
                                    out=ac[:], in0=u[:, j:j + 512], scalar=pv[:, PV_CW + j * 8 + ch:PV_CW + j * 8 + ch + 1], in1=ac[:],
                                    op0=ALU.mult, op1=ALU.add), reads=[uB, acB], writes=[acB])
                            S.op("act", lambda e, ac=ac, sg=sg: e.activation(out=sg[:], in_=ac[:], func=AF.Silu), reads=[acB], writes=[sgB])
                            if gi == 0:
                                if own:
                                    S.dma("sp", lambda e, sg=sg, c=c, ot0=ot0: e.dma_start(out=mqT[c, :, ot0:ot0 + 512], in_=sg[:]), reads=[sgB], writes=[mqB])
                            else:
                                S.dma("sp", lambda e, sg=sg, c=c, lt0=lt0: e.dma_start(out=mkT[c, :, lt0:lt0 + 512], in_=sg[:]), reads=[sgB], writes=[mkB])

            for q in range(4):
                S.dma("pool", lambda e, q=q: e.dma_start(out=wg[:, 4 * q:4 * q + 4, 0:8],
                                                         in_=w_in[q * 512:(q + 1) * 512, C_FF:C_FF + 8].rearrange("(kc p) c -> p kc c", p=P)), writes=[wgB])
                S.dma("pool", lambda e, q=q: e.dma_start(out=wg[:, 4 * q:4 * q + 4, 8:16],
                                                         in_=w_in[q * 512:(q + 1) * 512, C_MI:C_MI + 8].rearrange("(kc p) c -> p kc c", p=P)), writes=[wgB])
            for t in range(4):
                for gi, (c0, m) in enumerate(((0, 8), (8, 4), (12, 4))):
                    ps, psB = self.next_pbank()
                    for kc in range(16):
                        S.op("pe", lambda e, kc=kc, ps=ps, c0=c0, m=m, t=t: e.matmul(
                            ps[0:m, :], lhsT=wg[:, kc, c0:c0 + m], rhs=XN[:, kc, t * 512:(t + 1) * 512], start=(kc == 0), stop=(kc == 15)),
                            reads=[wgB, XNB[2 * t][kc], XNB[2 * t + 1][kc]], writes=[psB])
                    g_, g_B = gst[gi % 2], gstB[gi % 2]
                    S.op("act", lambda e, ps=ps, m=m, g_=g_: e.activation(out=g_[0:m, :], in_=ps[0:m, :], func=AF.Copy), reads=[psB], writes=[g_B])
                    S.dma("sp", lambda e, gi=gi, m=m, g_=g_, t=t: e.dma_start(out=gT[gi, 0:m, tok0 + t * 512:tok0 + (t + 1) * 512], in_=g_[0:m, :]),
                          reads=[g_B], writes=[gTB])


        inproj_pass(0, False)
        inproj_pass(2048, True)
        S.barrier()
        if self.stop_after == "A":
            return self.finish(outT)

        offB = PERS
        gt = [self.view(offB + i * 16384, [8, TL], F32) for i in range(7)]
        gtB = [Buf() for _ in range(7)]
        offB += 7 * 16384
        tabs = [self.view(offB + i * 2048, [P, 32, 16], F32) for i in range(2)]
        tabsB = [Buf() for _ in range(2)]
        offB += 2 * 2048
        ngT = self.view(offB, [P, 32, 8], F32)
        offB += 1024
        uT = self.view(offB, [P, 32, 4], F32)
        offB += 512
        crefB_ = self.view(offB, [P, 8, 16], F32)
        offB += 512
        rhoB_ = self.view(offB, [P, 4, 16], F32)
        offB += 256
        rhoR = self.view(offB, [4, 16], F32)
        offB += 64
        assert offB <= self.arena_bytes, offB
        ngTB, uTB, crefBB, rhoBB, rhoRB = Buf(), Buf(), Buf(), Buf(), Buf()
        A0, A1, A2, M0, M1, M2, M3 = range(7)
        T1, T2, T3, T4, T5, T6 = A1, A2, M0, M1, M2, M3
        for i, gi_, m_ in ((A0, 0, 8), (M0, 1, 4), (M1, 2, 4)):
            S.dma("sp", lambda e, i=i, gi_=gi_, m_=m_: e.dma_start(out=gt[i][0:m_, :], in_=gT[gi_, 0:m_, :]), reads=[gTB], writes=[gtB[i]])

        def softplus_neg(t, nrow, bcol):
            S.op("dve", lambda e: e.tensor_scalar(out=gt[t][0:nrow, :], in0=gt[t][0:nrow, :], scalar1=gb[0:nrow, bcol:bcol + 1], scalar2=None, op0=ALU.add),
                 reads=[gtB[t]], writes=[gtB[t]])
            S.op("act", lambda e: e.activation(out=gt[t][0:nrow, :], in_=gt[t][0:nrow, :], func=AF.Exp, scale=-1.0), reads=[gtB[t]], writes=[gtB[t]])
            S.op("act", lambda e: e.activation(out=gt[t][0:nrow, :], in_=gt[t][0:nrow, :], func=AF.Ln, bias=1.0, scale=1.0), reads=[gtB[t]], writes=[gtB[t]])
            S.op("dve", lambda e: e.tensor_scalar(out=gt[t][0:nrow, 0:2048], in0=gt[t][0:nrow, 0:2048], scalar1=pv[0:nrow, PV_FLAG:PV_FLAG + 1], scalar2=None, op0=ALU.mult),
                 reads=[gtB[t]], writes=[gtB[t]])

        def cumsum(dst, src, nrow):
            S.op("dve", lambda e: e.tensor_tensor_scan(out=gt[dst][0:nrow, :], data0=gt[src][0:nrow, :], data1=gt[src][0:nrow, :], initial=0.0,
                                                       op0=ALU.add, op1=ALU.max), reads=[gtB[src]], writes=[gtB[dst]])

        softplus_neg(A0, 8, 0)
        cumsum(T1, A0, 8)
        S.op("dve", lambda e: e.tensor_copy(out=gt[T2][:, :], in_=gt[T1][:, :]), reads=[gtB[T1]], writes=[gtB[T2]])
        S.op("dve", lambda e: e.tensor_scalar(out=gt[T2][:, 0:2048], in0=gt[T2][:, 0:2048], scalar1=pv[0:8, PV_PM:PV_PM + 1], scalar2=None, op0=ALU.add),
             reads=[gtB[T2]], writes=[gtB[T2]])
        S.op("dve", lambda e: e.tensor_scalar(out=gt[T3][0:4, :], in0=gt[T3][0:4, :], scalar1=gb[0:4, 1:2], scalar2=None, op0=ALU.add),
             reads=[gtB[T3]], writes=[gtB[T3]])
        S.op("dve", lambda e: e.tensor_scalar(out=gt[T3][0:4, 0:2048], in0=gt[T3][0:4, 0:2048], scalar1=pv[0:4, PV_PM:PV_PM + 1], scalar2=None, op0=ALU.add),
             reads=[gtB[T3]], writes=[gtB[T3]])
        softplus_neg(T4, 4, 2)
        cumsum(T5, T4, 4)
        S.op("dve", lambda e: e.tensor_tensor(out=gt[T3][0:4, :], in0=gt[T3][0:4, :], in1=gt[T5][0:4, :], op=ALU.add),
             reads=[gtB[T3], gtB[T5]], writes=[gtB[T3]])
        S.op("dve", lambda e: e.tensor_tensor_scan(out=gt[T6][0:4, :], data0=gt[T3][0:4, :], data1=gt[T3][0:4, :], initial=0.0, op0=ALU.max, op1=ALU.max),
             reads=[gtB[T3]], writes=[gtB[T6]])
        S.op("dve", lambda e: e.tensor_tensor(out=gt[T4][0:4, 2048:], in0=gt[T5][0:4, 2048:], in1=gt[T6][0:4, 2048:], op=ALU.subtract),
             reads=[gtB[T5], gtB[T6], gtB[T4]], writes=[gtB[T4]])
        S.op("act", lambda e: e.activation(out=gt[T4][0:4, 2048:], in_=gt[T4][0:4, 2048:], func=AF.Exp), reads=[gtB[T4]], writes=[gtB[T4]])
        S.dma("sp", lambda e: e.dma_start(out=lamem[:, 1, :], in_=gt[T4][0:4, 2048:]), reads=[gtB[T4]], writes=[lmB])
        S.op("dve", lambda e: e.tensor_copy(out=rhoR[:, :], in_=gt[T6][0:4, 2047:4095:128]), reads=[gtB[T6]], writes=[rhoRB])
        S.op("dve", lambda e: e.tensor_tensor(out=gt[A0][0:4, 0:2048].rearrange("r (j t) -> r j t", j=16),
                                               in0=rhoR[:, :].unsqueeze(2).to_broadcast([4, 16, 128]),
                                               in1=gt[T6][0:4, 2048:].rearrange("r (j t) -> r j t", j=16), op=ALU.subtract),
             reads=[rhoRB, gtB[T6], gtB[A0]], writes=[gtB[A0]])
        S.op("act", lambda e: e.activation(out=gt[A0][0:4, 0:2048], in_=gt[A0][0:4, 0:2048], func=AF.Exp), reads=[gtB[A0]], writes=[gtB[A0]])
        S.dma("sp", lambda e: e.dma_start(out=lamem[:, 0, :], in_=gt[A0][0:4, 0:2048]), reads=[gtB[A0]], writes=[lmB])
        tp, tpB = pb[6], pbB[6]
        tp2, tp2B = pb[7], pbB[7]
        for kb in range(32):
            S.op("pe", lambda e, kb=kb: e.transpose(tp[:, kb * 8:(kb + 1) * 8], gt[T2][0:8, kb * 128:(kb + 1) * 128], ident[0:8, 0:8]),
                 reads=[gtB[T2]], writes=[tpB])
            S.op("pe", lambda e, kb=kb: e.transpose(tp2[:, kb * 4:(kb + 1) * 4], gt[T3][0:4, kb * 128:(kb + 1) * 128], ident[0:4, 0:4]),
                 reads=[gtB[T3]], writes=[tp2B])
        S.op("dve", lambda e: e.tensor_copy(out=ngT[:].rearrange("p a b -> p (a b)"), in_=tp[:, 0:256]), reads=[tpB], writes=[ngTB])
        S.op("dve", lambda e: e.tensor_copy(out=uT[:].rearrange("p a b -> p (a b)"), in_=tp2[:, 0:128]), reads=[tp2B], writes=[uTB])
        bp, bpB = pb[5], pbB[5]
        for h in range(8):
            S.op("pe", lambda e, h=h: e.matmul(bp[:, h * 16:(h + 1) * 16], lhsT=sel[0:8, h, :], rhs=gt[T1][0:8, 2048 + 63:4096:128], start=True, stop=True),
                 reads=[gtB[T1]], writes=[bpB])
        for h in range(4):
            S.op("pe", lambda e, h=h: e.matmul(bp[:, 128 + h * 16:128 + (h + 1) * 16], lhsT=sel[0:4, h, :], rhs=rhoR[0:4, :], start=True, stop=True),
                 reads=[rhoRB], writes=[bpB])
        S.op("dve", lambda e: e.tensor_copy(out=crefB_[:].rearrange("p a b -> p (a b)"), in_=bp[:, 0:128]), reads=[bpB], writes=[crefBB])
        S.op("dve", lambda e: e.tensor_copy(out=rhoB_[:].rearrange("p a b -> p (a b)"), in_=bp[:, 128:192]), reads=[bpB], writes=[rhoBB])
        for h in range(8):
            tb, tbB = tabs[h % 2], tabsB[h % 2]
            S.op("dve", lambda e, h=h, tb=tb: e.tensor_tensor(out=tb[:], in0=ngT[:, :, h:h + 1].to_broadcast([P, 32, 16]),
                                                             in1=crefB_[:, h:h + 1, :].to_broadcast([P, 32, 16]), op=ALU.subtract),
                 reads=[ngTB, crefBB], writes=[tbB])
            S.op("dve", lambda e, tb=tb: e.tensor_scalar(out=tb[:], in0=tb[:], scalar1=60.0, scalar2=None, op0=ALU.min), reads=[tbB], writes=[tbB])
            S.op("act", lambda e, tb=tb: e.activation(out=tb[:], in_=tb[:], func=AF.Exp), reads=[tbB], writes=[tbB])
            S.dma("sp", lambda e, h=h, tb=tb: e.dma_start(out=Etab[h, :, :], in_=tb[:].rearrange("p a b -> p (a b)")), reads=[tbB], writes=[EtB])
        for h in range(4):
            tb, tbB = tabs[h % 2], tabsB[h % 2]
            S.op("dve", lambda e, h=h, tb=tb: e.tensor_tensor(out=tb[:], in0=uT[:, :, h:h + 1].to_broadcast([P, 32, 16]),
                                                             in1=rhoB_[:, h:h + 1, :].to_broadcast([P, 32, 16]), op=ALU.subtract),
                 reads=[uTB, rhoBB], writes=[tbB])
            S.op("dve", lambda e, tb=tb: e.tensor_scalar(out=tb[:], in0=tb[:], scalar1=60.0, scalar2=None, op0=ALU.min), reads=[tbB], writes=[tbB])
            S.op("act", lambda e, tb=tb: e.activation(out=tb[:], in_=tb[:], func=AF.Exp, bias=LN_SC), reads=[tbB], writes=[tbB])
            S.dma("sp", lambda e, h=h, tb=tb: e.dma_start(out=Wtab[h, :, :], in_=tb[:].rearrange("p a b -> p (a b)")), reads=[tbB], writes=[WtB])
        S.barrier()
        if self.stop_after == "B":
            return self.finish(outT)
        self.nbase = 6
        offC = PERS
        RES = self.view(offC, [P, 16, 1024], F32)
        slot_off = [offC, offC + 32768]
        offC += 65536
        bufA = self.view(offC, [P, 16, 1024], BF16)
        offC += 32768
        bufB = self.view(offC, [P, 16, 1024], BF16)
        offC += 32768
        assert offC <= self.arena_bytes, offC
        resB = [[Buf() for _ in range(2)] for _ in range(16)]
        bAB = [[Buf() for _ in range(2)] for _ in range(16)]
        bBB = [[Buf() for _ in range(2)] for _ in range(16)]
        ex = [self.stage[0], self.stage[1], xb16[0]]
        exB = [self.stageB[0], self.stageB[1], Buf()]
        pts = [self.stage[2], self.stage[3], self.sqh[0], xb16[1]]
        ptsB = [self.stageB[2], self.stageB[3], self.sqhB[0], Buf()]

        def slot_views(sl):
            o = slot_off[sl]
            Q = self.view(o, [P, 1024], BF16)
            K = self.view(o + 2048, [P, 4096], BF16)
            V = self.view(o + 10240, [P, 32, 256], BF16)
            Vf = self.view(o + 10240, [P, 32, 128], BF16)
            tab = self.view(o + 26624, [P, 32, 16], F32)
            lmr = self.view(o + 28672, [2, 1024], F32)
            return Q, K, V, Vf, tab, lmr
        slotB = [Buf(), Buf()]

        def linear_T(w, r0, nkc, col0s, src, srcB, evac, pre=None):
            for cg, c0 in enumerate(col0s):
                if pre is not None and cg < len(pre):
                    wt, wb = pre[cg]
                else:
                    wt, wb = self.load_w(w, r0, nkc, c0, 512)
                for t in range(2):
                    for c in range(4):
                        ps, psB = self.next_pbank()
                        for kc in range(nkc):
                            S.op("pe", lambda e, kc=kc, ps=ps, wt=wt, c=c, t=t: e.matmul(
                                ps[:, :], lhsT=wt[:, kc, c * 128:(c + 1) * 128], rhs=src[:, kc, t * 512:(t + 1) * 512], start=(kc == 0), stop=(kc == nkc - 1)),
                                reads=[wb, srcB[kc][t]], writes=[psB])
                        evac(cg, c, t, ps, psB)

        def evac_res_add(cg, c, t, ps, psB):
            kc = 4 * cg + c
            S.op("dve", lambda e, kc=kc, t=t, ps=ps: e.tensor_tensor(out=RES[:, kc, t * 512:(t + 1) * 512], in0=ps[:, :], in1=RES[:, kc, t * 512:(t + 1) * 512], op=ALU.add),
                 reads=[psB, resB[kc][t]], writes=[resB[kc][t]])

        def rms_res(gcol, dst, dstB):
            for t in range(2):
                self.rms_T(RES[:, :, t * 512:(t + 1) * 512], lambda kc, t=t: [resB[kc][t]], 512, gcol,
                           lambda kc, t=t: dst[:, kc, t * 512:(t + 1) * 512], lambda kc, t=t: [dstB[kc][t]], 1.0 / D)

        mx = self.view(PERS, [P, 16, 256], F32)
        mxB = Buf()
        mn = bufB[:, :, 0:256]
        mnB = [Buf() for _ in range(16)]
        for q in range(4):
            S.dma("sp", lambda e, q=q: e.dma_start(out=mx[:, q * 4:(q + 1) * 4, :], in_=memT[q * 512:(q + 1) * 512, :].rearrange("(kc p) t -> p kc t", p=P)), writes=[mxB])
        self.rms_T(mx[:, :, :], lambda kc: [mxB], 256, PV_MEM, lambda kc: bufB[:, kc, 0:256], lambda kc: [mnB[kc]], 1.0 / D)
        for hq in range(4):
            wt, wb = self.load_w(w_xkv, 0, 16, hq * 512, 512)
            pss = []
            for c in range(4):
                ps, psB = self.next_pbank()
                pss.append((ps, psB))
                for kc in range(16):
                    S.op("pe", lambda e, kc=kc, ps=ps, wt=wt, c=c: e.matmul(ps[:, 0:256], lhsT=wt[:, kc, c * 128:(c + 1) * 128], rhs=bufB[:, kc, 0:256],
                                                                           start=(kc == 0), stop=(kc == 15)), reads=[wb, mnB[kc]], writes=[psB])
            pbn, pbnB = self.next_nbank()
            for c in range(4):
                ps, psB = pss[c]
                sg, sgB = self.stage[c], self.stageB[c]
                S.op("act", lambda e, ps=ps, sg=sg: e.activation(out=sg[:, 0:256], in_=ps[:, 0:256], func=AF.Square), reads=[psB], writes=[sgB])
                S.op("pe", lambda e, sg=sg, c=c: e.matmul(pbn[:, 0:256], lhsT=self.ones[:], rhs=sg[:, 0:256], start=(c == 0), stop=(c == 3)), reads=[sgB], writes=[pbnB])
            S.op("act", lambda e: e.activation(out=tmpf[2][:, 0:256], in_=pbn[:, 0:256], func=AF.Ln, bias=self.epsc[:, 0:1], scale=1.0 / 512), reads=[pbnB], writes=[tmpfB[2]])
            S.op("act", lambda e: e.activation(out=tmpf[3][:, 0:256], in_=tmpf[2][:, 0:256], func=AF.Exp, scale=-0.5), reads=[tmpfB[2]], writes=[tmpfB[3]])
            for c in range(4):
                ps, psB = pss[c]
                S.op("dve", lambda e, ps=ps, c=c, hq=hq: e.scalar_tensor_tensor(out=kmem[:, 4 * hq + c, :], in0=ps[:, 0:256], scalar=pv[:, PV_XK + c:PV_XK + c + 1],
                                                                               in1=tmpf[3][:, 0:256], op0=ALU.mult, op1=ALU.mult),
                     reads=[psB, tmpfB[3]], writes=[kmemB[hq]])
        for gv in range(4):
            wt, wb = self.load_w(w_xkv, 0, 16, 2048 + gv * 512, 512)
            for mb in range(2):
                ps, psB = self.next_pbank()
                for kc in range(16):
                    S.op("pe", lambda e, kc=kc, ps=ps, wt=wt, mb=mb: e.matmul(ps[:, :], lhsT=bufB[:, kc, mb * 128:(mb + 1) * 128], rhs=wt[:, kc, :],
                                                                             start=(kc == 0), stop=(kc == 15)), reads=[wb, mnB[kc]], writes=[psB])
                S.op("act", lambda e, ps=ps, mb=mb, gv=gv: e.activation(out=vmem[:, mb, gv * 512:(gv + 1) * 512], in_=ps[:, :], func=AF.Copy), reads=[psB], writes=[vmemB])
        S.barrier()

        an = [0, False]
        FS = DeferS(S)

        def do_half(hf):
            o0 = hf * 1024
            self.nbase, self.nmod = 7, 1
            pre_wout = [self.load_w(w_out, 0, 16, c0_, 512) for c0_ in (0, 512)]
            def load_head(hd):
                fox = hd < 8
                hm = hd - 8
                sl = an[0] % 2
                an[0] += 1
                Q, K, V, Vf, tab, lmr = slot_views(sl)
                sB = slotB[sl]
                nk = 16 + 8 * (hf + 1)
                if fox:
                    S.dma("sp", lambda e, Q=Q, hd=hd: e.dma_start(out=Q[:, :], in_=qfT[hd, :, o0:o0 + 1024]), reads=[qfB], writes=[sB])
                    S.dma("sp", lambda e, K=K, hd=hd: e.dma_start(out=K[:, 0:nk * 128], in_=kfT[hd, :, 0:nk * 128]), reads=[kfB], writes=[sB])
                    S.dma("sp", lambda e, Vf=Vf, hd=hd: e.dma_start(out=Vf[:, 0:nk, :], in_=vf[hd, :, 0:nk, :]), reads=[vfB], writes=[sB])
                    S.dma("sp", lambda e, tab=tab, hd=hd: e.dma_start(out=tab[:].rearrange("p a b -> p (a b)"), in_=Etab[hd, :, :]), reads=[EtB], writes=[sB])
                else:
                    S.dma("sp", lambda e, Q=Q, hm=hm: e.dma_start(out=Q[:, :], in_=mqT[hm, :, o0:o0 + 1024]), reads=[mqB], writes=[sB])
                    S.dma("sp", lambda e, K=K, hm=hm: e.dma_start(out=K[:, 0:nk * 128], in_=mkT[hm, :, 0:nk * 128]), reads=[mkB], writes=[sB])
                    S.dma("sp", lambda e, V=V, hm=hm: e.dma_start(out=V[:, 0:nk, :], in_=mvv[hm, :, 0:nk, :]), reads=[mvB], writes=[sB])
                    S.dma("sp", lambda e, tab=tab, hm=hm: e.dma_start(out=tab[:].rearrange("p a b -> p (a b)"), in_=Wtab[hm, :, :]), reads=[WtB], writes=[sB])
                    S.dma("sp", lambda e, lmr=lmr, hm=hm: e.dma_start(out=lmr[:, :], in_=lamem[hm, :, o0:o0 + 1024]), reads=[lmB], writes=[sB])
                return (hd, hm, fox, Q, K, V, Vf, tab, lmr, sB)

            def attn_group(gl, hd, hm, fox, Q, K, V, Vf, tab, lmr, sB):
                g = 2 * hf + gl
                nkb = 16 + 4 * (g + 1)
                it = an[0] * 2 + gl
                O0, O0B = pb[2], pbB[2]
                O1, O1B = pb[3], pbB[3]
                Dn, DnB = pb[4], pbB[4]
                sring = (0, 1, 6, 5, 3) if fox else (0, 1, 6, 5)
                def emit_S(kb):
                    jd = kb - (16 + 4 * g)
                    c0 = max(0, jd) * 128
                    busy_ = self.srecent[-3:]
                    for t_ in range(len(sring)):
                        bi_ = sring[(self.pbn + t_) % len(sring)]
                        if bi_ not in busy_:
                            break
                    self.pbn += 1
                    self.srecent.append(bi_)
                    psS, psSB = pb[bi_], pbB[bi_]
                    S.op("pe", lambda e, psS=psS, kb=kb, c0=c0: e.matmul(
                        psS[:, c0:512], lhsT=K[:, kb * 128:(kb + 1) * 128], rhs=Q[:, gl * 512 + c0:(gl + 1) * 512], start=True, stop=True),
                        reads=[sB], writes=[psSB])
                    return psS, psSB

                def emit_POD(kb, psS, psSB):
                    jd = kb - (16 + 4 * g)
                    c0 = max(0, jd) * 128
                    nb_ = (512 - c0) // 128
                    pt, ptB = pts[self.stn % 4], ptsB[self.stn % 4]
                    self.stn += 1
                    jt0 = 4 * g + c0 // 128
                    if fox:
                        exi, exiB = ex[kb % 3], exB[kb % 3]
                        S.op("act", lambda e: e.activation(out=exi[:, c0:512], in_=psS[:, c0:512], func=AF.Exp, scale=128.0 ** -0.5),
                             reads=[psSB], writes=[exiB])
                        src, srcB_ = exi, exiB
                    else:
                        src, srcB_ = psS, psSB
                    cr = c0
                    nr = nb_
                    if jd >= 0:
                        S.op("dve", lambda e, jt0=jt0: e.scalar_tensor_tensor(out=pt[:, c0:c0 + 128], in0=src[:, c0:c0 + 128], scalar=tab[:, kb, jt0:jt0 + 1],
                                                                     in1=tri[:, :], op0=ALU.mult, op1=ALU.mult),
                             reads=[srcB_, sB], writes=[ptB])
                        cr, nr, jt0 = c0 + 128, nb_ - 1, jt0 + 1
                    if nr > 0:
                        tsl = tab[:, kb, jt0:jt0 + nr].unsqueeze(2).to_broadcast([P, nr, 128])
                        S.op("dve", lambda e: e.tensor_tensor(
                            out=pt[:, cr:512].rearrange("p (a b) -> p a b", a=nr), in0=src[:, cr:512].rearrange("p (a b) -> p a b", a=nr), in1=tsl, op=ALU.mult),
                            reads=[srcB_, sB], writes=[ptB], nowaw=(jd >= 0))
                    first, last = (kb == 0), (kb == nkb - 1)
                    if fox:
                        S.op("pe", lambda e: e.matmul(O0[:, c0:512], lhsT=Vf[:, kb, :], rhs=pt[:, c0:512], start=first, stop=last), reads=[sB, ptB], writes=[O0B])
                    else:
                        S.op("pe", lambda e: e.matmul(O0[:, c0:512], lhsT=V[:, kb, 0:128], rhs=pt[:, c0:512], start=first, stop=last), reads=[sB, ptB], writes=[O0B])
                        S.op("pe", lambda e: e.matmul(O1[:, c0:512], lhsT=V[:, kb, 128:256], rhs=pt[:, c0:512], start=first, stop=last), reads=[sB, ptB], writes=[O1B])
                    S.op("pe", lambda e: e.matmul(Dn[:, c0:512], lhsT=self.ones[:], rhs=pt[:, c0:512], start=first, stop=last), reads=[ptB], writes=[DnB])

                def fin():
                    cols = slice(gl * 512, (gl + 1) * 512)
                    if fox:
                        S.op("act", lambda e: e.activation(out=tmpf[4][:], in_=Dn[:, :], func=AF.Ln), reads=[DnB], writes=[tmpfB[4]])
                        S.op("dve", lambda e: e.tensor_copy(out=tmpf[5][:], in_=O0[:, :]), reads=[O0B], writes=[tmpfB[5]])
                        FS.op("act", lambda e: e.activation(out=tmpf[4][:], in_=tmpf[4][:], func=AF.Exp, scale=-1.0), reads=[tmpfB[4]], writes=[tmpfB[4]])
                        FS.op("dve", lambda e: e.tensor_tensor(out=tmpf[5][:], in0=tmpf[5][:], in1=tmpf[4][:], op=ALU.mult), reads=[tmpfB[5], tmpfB[4]], writes=[tmpfB[5]])
                        self.headnorm(tmpf[5][:], tmpfB[5], 512, PV_FO + hd, bufA[:, hd, cols], bAB[hd][gl], sq=self.sqh[1], sqB=self.sqhB[1], S=FS)
                    else:
                        FS.drain()
                        S.op("act", lambda e: e.activation(out=tmpf[6][:], in_=Dn[:, :], func=AF.Copy), reads=[DnB], writes=[tmpfB[6]])
                        S.op("act", lambda e: e.activation(out=tmpf[7][:], in_=O0[:, :], func=AF.Copy), reads=[O0B], writes=[tmpfB[7]])
                        S.op("act", lambda e: e.activation(out=tmpf[8][:], in_=O1[:, :], func=AF.Copy), reads=[O1B], writes=[tmpfB[8]])
                        for r_, ti in ((0, 4), (1, 5)):
                            pbn, pbnB = self.next_nbank()
                            S.op("pe", lambda e, pbn=pbn, r_=r_, lmr=lmr, cols=cols: e.matmul(pbn[:, :], lhsT=sel[0:2, r_, :], rhs=lmr[0:2, cols], start=True, stop=True),
                                 reads=[sB], writes=[pbnB])
                            S.op("act", lambda e, pbn=pbn, ti=ti: e.activation(out=tmpf[ti][:], in_=pbn[:, :], func=AF.Copy), reads=[pbnB], writes=[tmpfB[ti]])
                        FS.op("dve", lambda e: e.tensor_tensor(out=tmpf[6][:], in0=tmpf[6][:], in1=tmpf[4][:], op=ALU.mult), reads=[tmpfB[6], tmpfB[4]], writes=[tmpfB[6]])
                        FS.op("act", lambda e: e.activation(out=tmpf[6][:], in_=tmpf[6][:], func=AF.Abs), reads=[tmpfB[6]], writes=[tmpfB[6]])
                        FS.op("dve", lambda e: e.tensor_tensor(out=tmpf[6][:], in0=tmpf[6][:], in1=tmpf[5][:], op=ALU.max), reads=[tmpfB[6], tmpfB[5]], writes=[tmpfB[6]])
                        FS.op("act", lambda e: e.activation(out=tmpf[6][:], in_=tmpf[6][:], func=AF.Ln), reads=[tmpfB[6]], writes=[tmpfB[6]])
                        FS.op("act", lambda e: e.activation(out=tmpf[6][:], in_=tmpf[6][:], func=AF.Exp, scale=-1.0), reads=[tmpfB[6]], writes=[tmpfB[6]])
                        FS.op("dve", lambda e: e.tensor_tensor(out=tmpf[6][:], in0=tmpf[6][:], in1=tmpf[4][:], op=ALU.mult), reads=[tmpfB[6], tmpfB[4]], writes=[tmpfB[6]])
                        pbn, pbnB = self.next_nbank()
                        for c in range(2):
                            FS.op("dve", lambda e, c=c: e.tensor_tensor(out=tmpf[7 + c][:], in0=tmpf[7 + c][:], in1=tmpf[6][:], op=ALU.mult),
                                 reads=[tmpfB[7 + c], tmpfB[6]], writes=[tmpfB[7 + c]])
                            sq_, sq_B = (self.sqh[1], self.sqhB[1]) if c == 0 else (ex[0], exB[0])
                            FS.op("act", lambda e, c=c, sq_=sq_: e.activation(out=sq_[:], in_=tmpf[7 + c][:], func=AF.Square), reads=[tmpfB[7 + c]], writes=[sq_B])
                            FS.op("pe", lambda e, c=c, sq_=sq_, pbn=pbn: e.matmul(pbn[:, :], lhsT=self.ones[:], rhs=sq_[:], start=(c == 0), stop=(c == 1)), reads=[sq_B], writes=[pbnB])
                        FS.op("act", lambda e, pbn=pbn: e.activation(out=tmpf[2][:], in_=pbn[:, :], func=AF.Ln, bias=self.epsc[:, 0:1], scale=1.0 / 256), reads=[pbnB], writes=[tmpfB[2]])
                        FS.op("act", lambda e: e.activation(out=tmpf[3][:], in_=tmpf[2][:], func=AF.Exp, scale=-0.5), reads=[tmpfB[2]], writes=[tmpfB[3]])
                        for c in range(2):
                            ch = 2 * hm + c
                            mo_, mo_B = ex[1], exB[1]
                            FS.dma("sp", lambda e, ch=ch, mo_=mo_, gl=gl: e.dma_start(out=mo_[:], in_=mosT[ch, :, o0 + gl * 512:o0 + (gl + 1) * 512]), reads=[mosB], writes=[mo_B])
                            FS.op("dve", lambda e, c=c, ch=ch: e.scalar_tensor_tensor(out=tmpf[7 + c][:], in0=tmpf[7 + c][:], scalar=pv[:, PV_MO + ch:PV_MO + ch + 1],
                                                                                 in1=tmpf[3][:], op0=ALU.mult, op1=ALU.mult), reads=[tmpfB[7 + c], tmpfB[3]], writes=[tmpfB[7 + c]])
                            FS.op("dve", lambda e, c=c, ch=ch, mo_=mo_, cols=cols: e.tensor_tensor(out=bufA[:, 8 + ch, cols], in0=tmpf[7 + c][:], in1=mo_[:], op=ALU.mult),
                                 reads=[tmpfB[7 + c], mo_B], writes=[bAB[8 + ch][gl]])

                return dict(nkb=nkb, S=emit_S, POD=emit_POD, fin=fin, fox=fox)

            LA = 4
            glist = [(hd_, gl_) for hd_ in range(12) for gl_ in range(2)]
            heads = {0: load_head(0)}
            objs = {}

            def get_group(i_):
                if i_ not in objs:
                    hd_, gl_ = glist[i_]
                    if hd_ not in heads:
                        heads[hd_] = load_head(hd_)
                    objs[i_] = attn_group(gl_, *heads[hd_])
                return objs[i_]

            stream = []
            for i_ in range(len(glist)):
                hd_, gl_ = glist[i_]
                g_ = 2 * hf + gl_
                for kb_ in range(16 + 4 * (g_ + 1)):
                    stream.append((i_, kb_))
            pend = {}
            prev_fox = [True]

            def s_emit(j_):
                i_, kb_ = stream[j_]
                G = get_group(i_)
                pend[j_] = G["S"](kb_)

            for j_ in range(min(LA, len(stream))):
                s_emit(j_)
            for j_ in range(len(stream)):
                i_, kb_ = stream[j_]
                G = get_group(i_)
                if kb_ == 0:
                    hd_, gl_ = glist[i_]
                    if gl_ == 0 and hd_ + 1 < 12 and (hd_ + 1) not in heads:
                        heads[hd_ + 1] = load_head(hd_ + 1)
                    nper = (len(FS.q) + max(1, G["nkb"] - 8) - 1) // max(1, G["nkb"] - 8)
                if kb_ == 0:
                    if (not G["fox"]) and prev_fox[0]:
                        FS.drain()
                    prev_fox[0] = G["fox"]
                psS, psSB = pend.pop(j_)
                G["POD"](kb_, psS, psSB)
                if j_ + LA < len(stream):
                    s_emit(j_ + LA)
                if kb_ >= 2:
                    FS.drain(nper)
                if kb_ == G["nkb"] - 1:
                    FS.drain()
                    G["fin"]()
            FS.drain()
            if catT is not None:
                for kc in range(16):
                    S.dma("sp", lambda e, kc=kc: e.dma_start(out=catT[kc * P:(kc + 1) * P, o0:o0 + 1024], in_=bufA[:, kc, :]), reads=[bAB[kc][0], bAB[kc][1]])
            S.barrier()
            self.nbase, self.nmod = 6, 2
            self.pring = 6
            for q in range(4):
                S.dma("sp", lambda e, q=q: e.dma_start(out=RES[:, q * 4:(q + 1) * 4, :],
                                                       in_=xT[q * 512:(q + 1) * 512, 2048 + o0:2048 + o0 + 1024].rearrange("(kc p) t -> p kc t", p=P)),
                      writes=[resB[kc][t] for kc in range(4 * q, 4 * q + 4) for t in range(2)])
            linear_T(w_out, 0, 16, [0, 512, 1024, 1536], bufA, bAB, evac_res_add, pre=pre_wout)
            if x1T is not None:
                for kc in range(16):
                    S.dma("sp", lambda e, kc=kc: e.dma_start(out=x1T[kc * P:(kc + 1) * P, o0:o0 + 1024], in_=RES[:, kc, :]), reads=[resB[kc][0], resB[kc][1]])
            rms_res(PV_XAT, bufB, bBB)
            for hq in range(4):
                wt, wb = self.load_w(w_xq, 0, 16, hq * 512, 512)
                for t in range(2):
                    pss = []
                    for c in range(4):
                        ps, psB = self.next_pbank()
                        pss.append((ps, psB))
                        for kc in range(16):
                            S.op("pe", lambda e, kc=kc, ps=ps, wt=wt, c=c, t=t: e.matmul(ps[:, :], lhsT=wt[:, kc, c * 128:(c + 1) * 128], rhs=bufB[:, kc, t * 512:(t + 1) * 512],
                                                                                   start=(kc == 0), stop=(kc == 15)), reads=[wb, bBB[kc][t]], writes=[psB])
                    pbn, pbnB = self.next_nbank()
                    for c in range(4):
                        ps, psB = pss[c]
                        sg, sgB = self.stage[c], self.stageB[c]
                        S.op("act", lambda e, ps=ps, sg=sg: e.activation(out=sg[:], in_=ps[:, :], func=AF.Square), reads=[psB], writes=[sgB])
                        S.op("pe", lambda e, sg=sg, c=c, pbn=pbn: e.matmul(pbn[:, :], lhsT=self.ones[:], rhs=sg[:], start=(c == 0), stop=(c == 3)), reads=[sgB], writes=[pbnB])
                    S.op("act", lambda e, pbn=pbn: e.activation(out=tmpf[2][:], in_=pbn[:, :], func=AF.Ln, bias=self.epsc[:, 0:1], scale=1.0 / 512), reads=[pbnB], writes=[tmpfB[2]])
                    S.op("act", lambda e: e.activation(out=tmpf[3][:], in_=tmpf[2][:], func=AF.Exp, scale=-0.5), reads=[tmpfB[2]], writes=[tmpfB[3]])
                    for c in range(4):
                        ps, psB = pss[c]
                        S.op("dve", lambda e, ps=ps, c=c, hq=hq, t=t: e.scalar_tensor_tensor(out=bufA[:, 4 * hq + c, t * 512:(t + 1) * 512], in0=ps[:, :],
                                                                                         scalar=pv[:, PV_XQ + c:PV_XQ + c + 1], in1=tmpf[3][:], op0=ALU.mult, op1=ALU.mult),
                             reads=[psB, tmpfB[3]], writes=[bAB[4 * hq + c][t]])
            for hq in range(4):
                for t in range(2):
                    pms = []
                    for mb in range(2):
                        psS, psSB = self.next_pbank()
                        for c in range(4):
                            S.op("pe", lambda e, psS=psS, c=c, hq=hq, mb=mb, t=t: e.matmul(psS[:, :], lhsT=kmem[:, 4 * hq + c, mb * 128:(mb + 1) * 128],
                                                                                      rhs=bufA[:, 4 * hq + c, t * 512:(t + 1) * 512], start=(c == 0), stop=(c == 3)),
                                 reads=[kmemB[hq], bAB[4 * hq + c][t]], writes=[psSB])
                        pm_, pm_B = pts[mb], ptsB[mb]
                        S.op("act", lambda e, psS=psS, pm_=pm_: e.activation(out=pm_[:], in_=psS[:, :], func=AF.Exp, scale=512.0 ** -0.5), reads=[psSB], writes=[pm_B])
                        pms.append((pm_, pm_B))
                    pbn, pbnB = self.next_nbank()
                    for mb in range(2):
                        S.op("pe", lambda e, mb=mb, pbn=pbn, pms=pms: e.matmul(pbn[:, :], lhsT=self.ones[:], rhs=pms[mb][0][:], start=(mb == 0), stop=(mb == 1)),
                             reads=[pms[mb][1]], writes=[pbnB])
                    S.op("act", lambda e, pbn=pbn: e.activation(out=tmpf[4][:], in_=pbn[:, :], func=AF.Ln), reads=[pbnB], writes=[tmpfB[4]])
                    S.op("act", lambda e: e.activation(out=tmpf[4][:], in_=tmpf[4][:], func=AF.Exp, scale=-1.0), reads=[tmpfB[4]], writes=[tmpfB[4]])
                    for c in range(4):
                        ps, psB = self.next_pbank()
                        for mb in range(2):
                            S.op("pe", lambda e, ps=ps, mb=mb, c=c, hq=hq, pms=pms: e.matmul(ps[:, :], lhsT=vmem[:, mb, hq * 512 + c * 128:hq * 512 + (c + 1) * 128],
                                                                                        rhs=pms[mb][0][:], start=(mb == 0), stop=(mb == 1)),
                                 reads=[vmemB, pms[mb][1]], writes=[psB])
                        S.op("dve", lambda e, ps=ps, c=c, hq=hq, t=t: e.tensor_tensor(out=bufB[:, 4 * hq + c, t * 512:(t + 1) * 512], in0=ps[:, :], in1=tmpf[4][:], op=ALU.mult),
                             reads=[psB, tmpfB[4]], writes=[bBB[4 * hq + c][t]])
            linear_T(w_xo, 0, 16, [0, 512, 1024, 1536], bufB, bBB, evac_res_add)
            if x2T is not None:
                for kc in range(16):
                    S.dma("sp", lambda e, kc=kc: e.dma_start(out=x2T[kc * P:(kc + 1) * P, o0:o0 + 1024], in_=RES[:, kc, :]), reads=[resB[kc][0], resB[kc][1]])
            rms_res(PV_MLP, bufA, bAB)
            for qf in range(4):
                def evac_up(cg, c, t, ps, psB):
                    kc = 4 * cg + c
                    r_, r_B = tmpf[4 + (kc % 2)], tmpfB[4 + (kc % 2)]
                    S.op("act", lambda e, ps=ps, r_=r_: e.activation(out=r_[:], in_=ps[:, :], func=AF.Relu), reads=[psB], writes=[r_B])
                    S.op("dve", lambda e, r_=r_, kc=kc, t=t: e.tensor_tensor(out=bufB[:, kc, t * 512:(t + 1) * 512], in0=r_[:], in1=r_[:], op=ALU.mult),
                         reads=[r_B], writes=[bBB[kc][t]])
                linear_T(w_up, 0, 16, [qf * 2048 + i * 512 for i in range(4)], bufA, bAB, evac_up)
                linear_T(w_down, qf * 2048, 16, [0, 512, 1024, 1536], bufB, bBB, evac_res_add)
            for q in range(4):
                S.dma("sp", lambda e, q=q: e.dma_start(out=outT[q * 512:(q + 1) * 512, o0:o0 + 1024].rearrange("(kc p) t -> p kc t", p=P), in_=RES[:, q * 4:(q + 1) * 4, :]),
                      reads=[resB[kc][t] for kc in range(4 * q, 4 * q + 4) for t in range(2)])
            S.barrier()

        for hf_ in range(2):
            do_half(hf_)
        self.stop_after = None
        return self.finish(outT)

    def finish(self, outT):
        S = self.S
        if self.stop_after is not None:
            S.dma("sp", lambda e: e.dma_start(out=outT[0:P, 0:512], in_=self.tmpf[0][:]))
        self.S.emit(self.nc, self.st)
        self.st.close()
        return self.nc


def _prep_inputs(inputs):
    g = {k: np.asarray(v) for k, v in inputs.items()}
    x = g["x"]
    f32 = np.float32

    def col16(v):
        return np.ascontiguousarray(v.reshape(16, 128).T)

    pv = np.zeros((P, NPV), f32)
    pv[:, PV_MIX:PV_MIX + 16] = col16(g["mixer_norm"][0])
    pv[:, PV_XAT:PV_XAT + 16] = col16(g["xattn_norm"][0])
    pv[:, PV_MEM:PV_MEM + 16] = col16(g["mem_norm"][0])
    pv[:, PV_MLP:PV_MLP + 16] = col16(g["mlp_norm"][0])
    pv[:, PV_FQ] = g["fox_q_norm"][0]
    pv[:, PV_FK] = g["fox_k_norm"][0]
    pv[:, PV_FO:PV_FO + 8] = g["fox_out_norm"][0].reshape(8, 128).T
    pv[:, PV_MO:PV_MO + 8] = g["mlstm_out_norm"][0].reshape(8, 128).T
    pv[:, PV_XQ:PV_XQ + 4] = g["xq_norm"][0].reshape(4, 128).T
    pv[:, PV_XK:PV_XK + 4] = g["xk_norm"][0].reshape(4, 128).T
    pv[:, PV_CW:PV_CW + 32] = g["conv_w"][0].reshape(4, 8, 128).transpose(2, 0, 1).reshape(128, 32)
    pv[:, PV_CB:PV_CB + 8] = g["conv_b"][0].reshape(8, 128).T
    gb = np.zeros((8, 4), f32)
    gb[:, 0] = g["fox_f_bias"][0]
    gb[0:4, 1] = g["mlstm_i_bias"][0]
    gb[0:4, 2] = g["mlstm_f_bias"][0]
    shared = {
        "w_in": np.ascontiguousarray(g["w_in"][0]), "w_out": np.ascontiguousarray(g["w_out"][0]),
        "w_xq": np.ascontiguousarray(g["w_xq"][0]), "w_xkv": np.ascontiguousarray(g["w_xkv"][0]),
        "w_xo": np.ascontiguousarray(g["w_xo"][0]), "w_up": np.ascontiguousarray(g["w_up"][0]),
        "w_down": np.ascontiguousarray(g["w_down"][0]), "gb": gb,
    }
    in_maps = []
    for c in range(8):
        b, h = c // 2, c % 2
        xT = np.zeros((D, TL), f32)
        if h == 1:
            xT[:, 0:2048] = x[b, 0:2048].T
        xT[:, 2048:] = x[b, h * 2048:(h + 1) * 2048].T
        pvc = pv.copy()
        pvc[:, PV_FLAG] = 1.0 if h == 1 else 0.0
        pvc[:, PV_PM] = 0.0 if h == 1 else PMASK
        m = dict(shared)
        m["xT"] = xT
        m["memT"] = np.ascontiguousarray(g["mem"][b].T)
        m["pv"] = pvc
        in_maps.append(m)
    return in_maps


_CACHE = {}


def kernel(**inputs):
    in_maps = _prep_inputs(inputs)
    if "nc" not in _CACHE:
        _CACHE["nc"] = Builder().build()
    res = run_bass_kernel_spmd(_CACHE["nc"], in_maps, core_ids=list(range(8)))
    out = np.zeros((4, 4096, D), np.float32)
    for c in range(8):
        b, h = c // 2, c % 2
        out[b, h * 2048:(h + 1) * 2048, :] = res.results[c]["outT"].T
    return out
```
